# Optimizing a Trainium2 kernel written in Bass

```python
import jax, jax.numpy as jnp
from jax import lax
import numpy as np

D_MODEL = 1024
BATCH = 16
SEQ = 2048
DEPTH = 1

N_MEM = 256
HEAD_DIM = 64
NORM_EPS = 1e-6
ROPE_THETA = 10000.0
RW_HEADS = 8
RW_WIDTH = RW_HEADS * HEAD_DIM
DECAY_LORA = 64
AAA_LORA = 64
GATE_LORA = 128
RW_GN_EPS = 64e-5
RW_IN = 3 * RW_WIDTH + DECAY_LORA + AAA_LORA + GATE_LORA
NSA_HEADS = 8
NSA_KV_GROUPS = 2
NSA_HPG = NSA_HEADS // NSA_KV_GROUPS
NSA_WIDTH = NSA_HEADS * HEAD_DIM
KV_WIDTH = NSA_KV_GROUPS * HEAD_DIM
CMP_BLOCK = 32
CMP_STRIDE = 16
CMP_HIDDEN = 128
SEL_BLOCK = 64
SEL_TOPK = 8
WINDOW = 512
Q_CHUNK = 32
XA_HEADS = 4
XA_HEAD_DIM = D_MODEL // XA_HEADS
D_FF = -(-8 * D_MODEL // (3 * 256)) * 256
IN_SIZES = [RW_IN, NSA_WIDTH] + [KV_WIDTH] * 6 + [3 * NSA_HEADS, 2 * D_MODEL]
D_IN = RW_IN + NSA_WIDTH + 6 * KV_WIDTH + 3 * NSA_HEADS + 2 * D_MODEL

kernel_name = 'hybrid_rwkv7_nsa_block'


def split_cols(p, sizes):
    return jnp.split(p, np.cumsum(sizes)[:-1].tolist(), axis=-1)


def rms_norm(x, g):
    xf = x.astype(jnp.float32)
    y = xf * lax.rsqrt(jnp.mean(xf * xf, axis=-1, keepdims=True) + NORM_EPS)
    return (y * g.astype(jnp.float32)).astype(x.dtype)


def rope(x, pos):
    half = x.shape[-1] // 2
    inv_freq = ROPE_THETA ** (-jnp.arange(half, dtype=jnp.float32) / half)
    ang = pos.astype(jnp.float32)[:, None] * inv_freq[None, :]
    cos, sin = jnp.cos(ang)[:, None, :], jnp.sin(ang)[:, None, :]
    xf = x.astype(jnp.float32)
    x1, x2 = xf[..., :half], xf[..., half:]
    return jnp.concatenate([x1 * cos - x2 * sin, x2 * cos + x1 * sin], axis=-1).astype(x.dtype)


def masked_softmax(s, mask):
    s = jnp.where(mask, s.astype(jnp.float32), -1e30)
    e = jnp.where(mask, jnp.exp(s - jnp.max(s, axis=-1, keepdims=True)), 0.0)
    return e / jnp.maximum(jnp.sum(e, axis=-1, keepdims=True), 1e-30)


def token_shift(p, mu):
    prev = jnp.pad(p[:, :-1], ((0, 0), (1, 0), (0, 0)))
    return p + mu * (prev - p)


def rwkv7_time_mix(p_rw, w_up, w0, a_up, a0, g_up, k_k, k_a, r_k, ln_g, ln_b):
    B, T, _ = p_rw.shape
    f32 = jnp.float32
    r, k, v, xw, xa, xg = split_cols(p_rw, [RW_WIDTH] * 3 + [DECAY_LORA, AAA_LORA, GATE_LORA])
    w_log = -jax.nn.softplus(-(w0 + jnp.tanh(xw) @ w_up).astype(f32)) - 0.5
    decay = jnp.exp(-jnp.exp(w_log))
    a = jax.nn.sigmoid((a0 + xa @ a_up).astype(f32))
    g = jax.nn.sigmoid(xg) @ g_up

    def heads(t):
        return t.astype(f32).reshape(B, T, RW_HEADS, HEAD_DIM)

    r, k, v, decay, a = heads(r), heads(k), heads(v), heads(decay), heads(a)
    kk = k * k_k.astype(f32).reshape(RW_HEADS, HEAD_DIM)
    kk = kk * lax.rsqrt(jnp.maximum(jnp.sum(kk * kk, axis=-1, keepdims=True), 1e-12))
    k = k * (1.0 + (a - 1.0) * k_a.astype(f32).reshape(RW_HEADS, HEAD_DIM))

    def step(S, inp):
        r_t, w_t, k_t, v_t, kk_t, a_t = inp
        sa = jnp.einsum('bhij,bhj->bhi', S, -kk_t)
        S = S * w_t[:, :, None, :] + sa[..., :, None] * (kk_t * a_t)[..., None, :] \
            + v_t[..., :, None] * k_t[..., None, :]
        return S, jnp.einsum('bhij,bhj->bhi', S, r_t)

    xs = tuple(jnp.moveaxis(t, 1, 0) for t in (r, decay, k, v, kk, a))
    S0 = jnp.zeros((B, RW_HEADS, HEAD_DIM, HEAD_DIM), f32)
    _, ys = lax.scan(step, S0, xs)
    y = jnp.moveaxis(ys, 0, 1)
    mu = jnp.mean(y, axis=-1, keepdims=True)
    var = jnp.mean(jnp.square(y - mu), axis=-1, keepdims=True)
    y = ((y - mu) * lax.rsqrt(var + RW_GN_EPS)).reshape(B, T, RW_WIDTH)
    y = y * ln_g.astype(f32) + ln_b.astype(f32)
    bonus = (jnp.sum(r * k * r_k.astype(f32), axis=-1, keepdims=True) * v).reshape(B, T, RW_WIDTH)
    return ((y + bonus) * g.astype(f32)).astype(p_rw.dtype)


def nsa_attention(q, kc, vc, ks, vs, kw, vw, gate_logits, pe_k, pe_v, ck1, ck2, cv1, cv2):
    B, T, _ = q.shape
    G, HPG, D = NSA_KV_GROUPS, NSA_HPG, HEAD_DIM
    scale = D ** -0.5
    pos = jnp.arange(T)
    q = q.reshape(B, T, NSA_HEADS, D)
    q_rot = rope(q, pos).reshape(B, T, G, HPG, D)
    q_cmp = q.reshape(B, T, G, HPG, D)
    kc, vc, ks, vs, kw, vw = (t.reshape(B, T, G, D) for t in (kc, vc, ks, vs, kw, vw))
    ks, kw = rope(ks, pos), rope(kw, pos)

    n_cmp = (T - CMP_BLOCK) // CMP_STRIDE + 1
    cmp_start = np.arange(n_cmp) * CMP_STRIDE
    cmp_idx = cmp_start[:, None] + np.arange(CMP_BLOCK)[None, :]
    cmp_last = jnp.asarray(cmp_start + CMP_BLOCK - 1)

    def compress(t, pe, w1, w2):
        blk = t[:, cmp_idx] + pe[:, None, :]
        blk = jnp.transpose(blk, (0, 1, 3, 2, 4)).reshape(B, n_cmp, G, CMP_BLOCK * D)
        return jax.nn.gelu(blk @ w1) @ w2

    k_cmp = compress(kc, pe_k, ck1, ck2)
    v_cmp = compress(vc, pe_v, cv1, cv2)

    n_sel = T // SEL_BLOCK
    top_k = min(SEL_TOPK, n_sel)
    sel_start = np.arange(n_sel) * SEL_BLOCK
    overlap = np.clip(np.minimum(cmp_start[:, None] + CMP_BLOCK, sel_start[None, :] + SEL_BLOCK)
                      - np.maximum(cmp_start[:, None], sel_start[None, :]), 0, None)
    overlap = jnp.asarray(overlap / CMP_BLOCK, dtype=jnp.float32)
    ks_blk = jnp.transpose(ks.reshape(B, n_sel, SEL_BLOCK, G, D), (0, 3, 1, 2, 4))
    vs_blk = jnp.transpose(vs.reshape(B, n_sel, SEL_BLOCK, G, D), (0, 3, 1, 2, 4))
    kw_pad = jnp.pad(kw, ((0, 0), (WINDOW, 0), (0, 0), (0, 0)))
    vw_pad = jnp.pad(vw, ((0, 0), (WINDOW, 0), (0, 0), (0, 0)))
    b_ix = jnp.arange(B)[:, None, None, None]
    g_ix = jnp.arange(G)[None, :, None, None]
    blk_ids = jnp.arange(n_sel)
    span = Q_CHUNK + WINDOW

    def chunk(c):
        t0 = c * Q_CHUNK
        tq = t0 + jnp.arange(Q_CHUNK)
        qc = lax.dynamic_slice_in_dim(q_cmp, t0, Q_CHUNK, axis=1)
        qr = lax.dynamic_slice_in_dim(q_rot, t0, Q_CHUNK, axis=1)
        s = jnp.einsum('bcghd,bngd->bghcn', qc, k_cmp) * scale
        p_cmp = masked_softmax(s, cmp_last[None, :] <= tq[:, None])
        o_cmp = jnp.einsum('bghcn,bngd->bcghd', p_cmp.astype(v_cmp.dtype), v_cmp)
        imp = jnp.einsum('bghcn,nj->bgcj', p_cmp, overlap)
        cur = tq[:, None] // SEL_BLOCK
        forced = (blk_ids[None] == 0) | (blk_ids[None] == cur) | (blk_ids[None] == cur - 1)
        imp = jnp.where(forced, 1e4, jnp.where(blk_ids[None] <= cur, imp, -1.0))
        _, sel = lax.top_k(imp, top_k)
        kb = ks_blk[b_ix, g_ix, sel]
        vb = vs_blk[b_ix, g_ix, sel]
        s = jnp.einsum('bcghd,bgcksd->bghcks', qr, kb) * scale
        key_pos = sel[..., None] * SEL_BLOCK + jnp.arange(SEL_BLOCK)
        mask = (key_pos <= tq[None, None, :, None, None]).reshape(B, G, 1, Q_CHUNK, top_k * SEL_BLOCK)
        p = masked_softmax(s.reshape(B, G, HPG, Q_CHUNK, top_k * SEL_BLOCK), mask)
        o_sel = jnp.einsum('bghcks,bgcksd->bcghd', p.reshape(s.shape).astype(vb.dtype), vb)
        kwc = lax.dynamic_slice_in_dim(kw_pad, t0, span, axis=1)
        vwc = lax.dynamic_slice_in_dim(vw_pad, t0, span, axis=1)
        kpos = t0 - WINDOW + jnp.arange(span)
        wmask = (kpos[None] <= tq[:, None]) & (kpos[None] > tq[:, None] - WINDOW) & (kpos[None] >= 0)
        s = jnp.einsum('bcghd,bsgd->bghcs', qr, kwc) * scale
        p = masked_softmax(s, wmask)
        o_win = jnp.einsum('bghcs,bsgd->bcghd', p.astype(vwc.dtype), vwc)
        return o_cmp, o_sel, o_win

    outs = lax.map(chunk, jnp.arange(T // Q_CHUNK))
    o_cmp, o_sel, o_win = (jnp.moveaxis(o, 0, 1).reshape(B, T, NSA_HEADS, D) for o in outs)
    gates = jax.nn.sigmoid(gate_logits.astype(jnp.float32)).reshape(B, T, NSA_HEADS, 3)
    o = gates[..., 0:1] * o_cmp + gates[..., 1:2] * o_sel + gates[..., 2:3] * o_win
    return o.reshape(B, T, NSA_WIDTH).astype(q.dtype)


def hybrid_mixer(h_n, w_in, shift_mu, rw_w_up, rw_w0, rw_a_up, rw_a0, rw_g_up, rw_k_k, rw_k_a,
                 rw_r_k, rw_ln_g, rw_ln_b, nsa_pe_k, nsa_pe_v, nsa_ck1, nsa_ck2, nsa_cv1, nsa_cv2,
                 w_up_rw, w_up_nsa, w_out):
    p = h_n @ w_in
    p_rw, q, kc, vc, ks, vs, kw, vw, g_nsa, g_merge = split_cols(p, IN_SIZES)
    p_rw = token_shift(p_rw, shift_mu)
    y_rw = rwkv7_time_mix(p_rw, rw_w_up, rw_w0, rw_a_up, rw_a0, rw_g_up, rw_k_k, rw_k_a,
                          rw_r_k, rw_ln_g, rw_ln_b) @ w_up_rw
    y_nsa = nsa_attention(q, kc, vc, ks, vs, kw, vw, g_nsa, nsa_pe_k, nsa_pe_v,
                          nsa_ck1, nsa_ck2, nsa_cv1, nsa_cv2) @ w_up_nsa
    g_rw, g_ns = jnp.split(jax.nn.sigmoid(g_merge), 2, axis=-1)
    return (g_rw * y_rw + g_ns * y_nsa) @ w_out


def memory_cross_attention(h_n, mem_n, wq, wkv, wo):
    B, T, _ = h_n.shape
    M = mem_n.shape[1]
    q = (h_n @ wq).reshape(B, T, XA_HEADS, XA_HEAD_DIM)
    k, v = jnp.split(mem_n @ wkv, 2, axis=-1)
    k = k.reshape(B, M, XA_HEADS, XA_HEAD_DIM)
    v = v.reshape(B, M, XA_HEADS, XA_HEAD_DIM)
    s = jnp.einsum('bthd,bmhd->bhtm', q, k).astype(jnp.float32) * (XA_HEAD_DIM ** -0.5)
    p = jax.nn.softmax(s, axis=-1)
    o = jnp.einsum('bhtm,bmhd->bthd', p.astype(v.dtype), v).reshape(B, T, D_MODEL)
    return o @ wo


def swiglu_ffn(h_n, w_gu, w_down):
    g, u = jnp.split(h_n @ w_gu, 2, axis=-1)
    return (jax.nn.silu(g) * u) @ w_down


def setup_inputs(seed: int = 0) -> dict:
    key = jax.random.key(seed)
    keys = iter(jax.random.split(key, 40))
    L = (DEPTH,)

    def nrm(shape, scale):
        return jax.random.normal(next(keys), shape, jnp.float32) * scale

    def gain(n):
        return 1.0 + nrm(L + (n,), 0.02)

    inputs = {}
    inputs['x'] = nrm((BATCH, SEQ, D_MODEL), 1.0)
    inputs['mem'] = nrm((BATCH, N_MEM, D_MODEL), 1.0)
    inputs['norm_mix_g'] = gain(D_MODEL)
    inputs['w_in'] = nrm(L + (D_MODEL, D_IN), D_MODEL ** -0.5)
    inputs['shift_mu'] = jax.random.uniform(next(keys), L + (RW_IN,), jnp.float32)
    inputs['rw_w_up'] = nrm(L + (DECAY_LORA, RW_WIDTH), 0.5 * DECAY_LORA ** -0.5)
    inputs['rw_w0'] = -2.0 + nrm(L + (RW_WIDTH,), 0.5)
    inputs['rw_a_up'] = nrm(L + (AAA_LORA, RW_WIDTH), AAA_LORA ** -0.5)
    inputs['rw_a0'] = nrm(L + (RW_WIDTH,), 0.5)
    inputs['rw_g_up'] = nrm(L + (GATE_LORA, RW_WIDTH), GATE_LORA ** -0.5)
    inputs['rw_k_k'] = 0.85 + nrm(L + (RW_WIDTH,), 0.05)
    inputs['rw_k_a'] = 1.0 + nrm(L + (RW_WIDTH,), 0.05)
    inputs['rw_r_k'] = nrm(L + (RW_HEADS, HEAD_DIM), 0.1)
    inputs['rw_ln_g'] = gain(RW_WIDTH)
    inputs['rw_ln_b'] = nrm(L + (RW_WIDTH,), 0.02)
    inputs['nsa_pe_k'] = nrm(L + (CMP_BLOCK, HEAD_DIM), 0.02)
    inputs['nsa_pe_v'] = nrm(L + (CMP_BLOCK, HEAD_DIM), 0.02)
    inputs['nsa_ck1'] = nrm(L + (CMP_BLOCK * HEAD_DIM, CMP_HIDDEN), (CMP_BLOCK * HEAD_DIM) ** -0.5)
    inputs['nsa_ck2'] = nrm(L + (CMP_HIDDEN, HEAD_DIM), CMP_HIDDEN ** -0.5)
    inputs['nsa_cv1'] = nrm(L + (CMP_BLOCK * HEAD_DIM, CMP_HIDDEN), (CMP_BLOCK * HEAD_DIM) ** -0.5)
    inputs['nsa_cv2'] = nrm(L + (CMP_HIDDEN, HEAD_DIM), CMP_HIDDEN ** -0.5)
    inputs['w_up_rw'] = nrm(L + (RW_WIDTH, D_MODEL), RW_WIDTH ** -0.5)
    inputs['w_up_nsa'] = nrm(L + (NSA_WIDTH, D_MODEL), NSA_WIDTH ** -0.5)
    inputs['w_out'] = nrm(L + (D_MODEL, D_MODEL), D_MODEL ** -0.5)
    inputs['norm_xa_g'] = gain(D_MODEL)
    inputs['norm_mem_g'] = gain(D_MODEL)
    inputs['xa_wq'] = nrm(L + (D_MODEL, D_MODEL), D_MODEL ** -0.5)
    inputs['xa_wkv'] = nrm(L + (D_MODEL, 2 * D_MODEL), D_MODEL ** -0.5)
    inputs['xa_wo'] = nrm(L + (D_MODEL, D_MODEL), D_MODEL ** -0.5)
    inputs['norm_ffn_g'] = gain(D_MODEL)
    inputs['ffn_w_gu'] = nrm(L + (D_MODEL, 2 * D_FF), D_MODEL ** -0.5)
    inputs['ffn_w_down'] = nrm(L + (D_FF, D_MODEL), D_FF ** -0.5)
    inputs['final_norm_g'] = 1.0 + nrm((D_MODEL,), 0.02)
    return inputs


def reference(x, mem, norm_mix_g, w_in, shift_mu, rw_w_up, rw_w0, rw_a_up, rw_a0, rw_g_up,
              rw_k_k, rw_k_a, rw_r_k, rw_ln_g, rw_ln_b, nsa_pe_k, nsa_pe_v, nsa_ck1, nsa_ck2,
              nsa_cv1, nsa_cv2, w_up_rw, w_up_nsa, w_out, norm_xa_g, norm_mem_g, xa_wq, xa_wkv,
              xa_wo, norm_ffn_g, ffn_w_gu, ffn_w_down, final_norm_g):
    h = x
    for l in range(DEPTH):
        h = h + hybrid_mixer(rms_norm(h, norm_mix_g[l]), w_in[l], shift_mu[l], rw_w_up[l], rw_w0[l],
                             rw_a_up[l], rw_a0[l], rw_g_up[l], rw_k_k[l], rw_k_a[l], rw_r_k[l],
                             rw_ln_g[l], rw_ln_b[l], nsa_pe_k[l], nsa_pe_v[l], nsa_ck1[l], nsa_ck2[l],
                             nsa_cv1[l], nsa_cv2[l], w_up_rw[l], w_up_nsa[l], w_out[l])
        h = h + memory_cross_attention(rms_norm(h, norm_xa_g[l]), rms_norm(mem, norm_mem_g[l]),
                                       xa_wq[l], xa_wkv[l], xa_wo[l])
        h = h + swiglu_ffn(rms_norm(h, norm_ffn_g[l]), ffn_w_gu[l], ffn_w_down[l])
    return rms_norm(h, final_norm_g)
```

```python
import contextlib
import math
import numpy as np
import concourse.bass as bass
import concourse.mybir as mybir
from concourse.bass_utils import run_bass_kernel_spmd

F32 = mybir.dt.float32
BF16 = mybir.dt.bfloat16
AF = mybir.ActivationFunctionType
ALU = mybir.AluOpType

ENGS = ["pe", "act", "dve", "pool", "sp"]
SAME_SYNC = {"pe": False, "act": True, "dve": True, "pool": True, "sp": False}
N_DMA_SEMS = 40

D = 1024
NMEM = 256
DFF = 2816
EPS = 1e-6
C_RW, C_Q, C_KC, C_VC, C_KS, C_VS, C_KW, C_VW, C_GN, C_GM = 0, 1792, 2304, 2432, 2560, 2688, 2816, 2944, 3072, 3096
DIN = 5144
CG_MIX, CG_XA, CG_MEM, CG_FFN, CG_FIN, C_MU, C_W0, C_A0, C_KK, C_KA, C_RK, C_LNG, C_LNB, NCOLS = \
    0, 8, 16, 24, 32, 40, 54, 58, 62, 66, 70, 74, 78, 82


class Prog:
    def __init__(self, nc):
        self.nc = nc
        self.stack = contextlib.ExitStack()
        self.ops = {e: [] for e in ENGS}
        self.sems = []
        self.esem = {e: self._newsem("e_" + e) for e in ENGS}
        self.cnt = {e: 0 for e in ENGS}
        self.seen = {e: {} for e in ENGS}
        self.units = {}
        self.dma_sems = [self._newsem("d%d" % i) for i in range(N_DMA_SEMS)]
        self.dma_val = {k: 0 for k in self.dma_sems}
        self.dma_rr = 0
        self.out_events = []
        self.uid = 0
        self.defer = None

    def _newsem(self, name):
        h = self.stack.enter_context(self.nc.semaphore(name))
        self.sems.append(h)
        return len(self.sems) - 1

    def sbuf(self, shape, dtype, name=None):
        self.uid += 1
        return self.stack.enter_context(self.nc.sbuf_tensor(name or ("sb%d" % self.uid), list(shape), dtype))

    def psum(self, shape, dtype, name=None):
        self.uid += 1
        return self.stack.enter_context(self.nc.psum_tensor(name or ("ps%d" % self.uid), list(shape), dtype))

    def _unit(self, u):
        st = self.units.get(u)
        if st is None:
            st = {"w": {}, "r": {}}
            self.units[u] = st
        return st

    def capture(self, thunk):
        assert self.defer is None
        self.defer = []
        thunk()
        ops, self.defer = self.defer, None
        return ops

    def replay(self, ops):
        for a in ops:
            self.op(*a)

    def op(self, eng, fn, reads=(), writes=(), dma=False, is_out=False, par=False):
        if self.defer is not None:
            self.defer.append((eng, fn, tuple(reads), tuple(writes), dma, is_out, par))
            return None
        waits = {}
        own = self.esem[eng]

        def need(ev):
            if ev is None:
                return
            sk, v = ev
            if sk == own and not dma and not SAME_SYNC[eng]:
                return
            if self.seen[eng].get(sk, 0) >= v:
                return
            if waits.get(sk, 0) < v:
                waits[sk] = v

        for u in reads:
            for sk, v in self._unit(u)["w"].items():
                need((sk, v))
        for u in writes:
            st = self._unit(u)
            if not par:
                for sk, v in st["w"].items():
                    need((sk, v))
            for sk, v in st["r"].items():
                need((sk, v))
        if dma:
            k = self.dma_sems[self.dma_rr % len(self.dma_sems)]
            self.dma_rr += 1
            need((k, self.dma_val[k]))
            self.dma_val[k] += 16
            ev = (k, self.dma_val[k])
            inc = (k, 16)
        else:
            self.cnt[eng] += 1
            ev = (own, self.cnt[eng])
            inc = (own, 1)
        for sk, v in waits.items():
            self.seen[eng][sk] = v
        self.ops[eng].append((fn, sorted(waits.items()), inc))
        for u in reads:
            st = self._unit(u)
            if st["r"].get(ev[0], 0) < ev[1]:
                st["r"][ev[0]] = ev[1]
        for u in writes:
            st = self._unit(u)
            if par:
                st["w"][ev[0]] = ev[1]
            else:
                st["w"] = {ev[0]: ev[1]}
            st["r"] = {}
        if is_out:
            self.out_events.append(ev)
        return ev

    def barrier(self, keep=()):
        kept = {u: self.units[u] for u in keep if u in self.units}
        skip = set()
        for st in kept.values():
            for sk, v in st["w"].items():
                if sk in self.dma_val and self.dma_val[sk] == v:
                    skip.add(sk)
        allw = {}
        for e in ENGS:
            if self.cnt[e] > 0:
                allw[self.esem[e]] = self.cnt[e]
        for k in self.dma_sems:
            if self.dma_val[k] > 0 and k not in skip:
                allw[k] = self.dma_val[k]
        for e in ENGS:
            waits = {}
            for sk, v in allw.items():
                if sk == self.esem[e] and not SAME_SYNC[e]:
                    continue
                if self.seen[e].get(sk, 0) < v:
                    waits[sk] = v
                    self.seen[e][sk] = v
            if waits:
                self.ops[e].append((None, sorted(waits.items()), None))
        self.units = kept

    def finish(self):
        waits = {}
        for sk, v in self.out_events:
            if waits.get(sk, 0) < v:
                waits[sk] = v
        self.ops["sp"].append((None, sorted(waits.items()), None))

    def emit(self):
        nc = self.nc
        sems = self.sems
        ops = self.ops

        def run(engobj, lst):
            for fn, waits, inc in lst:
                for sk, v in waits:
                    engobj.wait_ge(sems[sk], v)
                if fn is None:
                    continue
                ins = fn(engobj)
                if inc is not None:
                    ins.then_inc(sems[inc[0]], inc[1])

        with nc.Block() as block:
            @block.tensor
            def _(e):
                run(e, ops["pe"])

            @block.scalar
            def _(e):
                run(e, ops["act"])

            @block.vector
            def _(e):
                run(e, ops["dve"])

            @block.gpsimd
            def _(e):
                run(e, ops["pool"])

            @block.sync
            def _(e):
                run(e, ops["sp"])
        self.stack.close()


class Arena:
    def __init__(self, P, nbytes):
        self.t = P.sbuf([128, nbytes // 4], F32, "arena")
        self.n = nbytes
        self.off = 0
        self.top = nbytes

    def mark(self):
        return self.off

    def release(self, m):
        self.off = m

    def alloc(self, free_shape, dtype, top=False):
        n = 1
        for d in free_shape:
            n *= d
        b = n * (4 if dtype == F32 else 2)
        b = (b + 63) // 64 * 64
        assert self.off + b <= self.top, ("arena overflow", self.off, b, self.top)
        if top:
            self.top -= b
            v = self.t[:, self.top // 4:(self.top + b) // 4]
        else:
            v = self.t[:, self.off // 4:(self.off + b) // 4]
            self.off += b
        if dtype != F32:
            v = v.bitcast(dtype)
        v = v[:, 0:n]
        if len(free_shape) == 2:
            v = v.rearrange("p (a b) -> p a b", a=free_shape[0])
        elif len(free_shape) == 3:
            v = v.rearrange("p (a b c) -> p a b c", a=free_shape[0], b=free_shape[1])
        elif len(free_shape) == 4:
            v = v.rearrange("p (a b c d) -> p a b c d", a=free_shape[0], b=free_shape[1], c=free_shape[2])
        return v


class Builder:
    def __init__(self, nc, T, NSEQ, dbg=None):
        self.nc = nc
        self.T = T
        self.NSEQ = NSEQ
        self.dbg = dbg or {}
        self.P = Prog(nc)
        self.A = Arena(self.P, 204 * 1024)
        self.PS = [self.P.psum([128, 512], F32, "bank%d" % i)[:, :] for i in range(8)]
        self.bank_rr = 0
        self.nrr = 8
        self.bank_set = None
        self.par = False
        self.acc_rr = 0
        self.uid = 0
        self.dq = 0

    def bank(self):
        if self.bank_set is not None:
            bs = self.bank_set
            i = bs[self.bank_rr % len(bs)]
        else:
            i = self.bank_rr % self.nrr
        self.bank_rr += 1
        return i, self.PS[i], "ps%d" % i

    def acc_bank(self):
        i = 6 + (self.acc_rr % 2)
        self.acc_rr += 1
        return i, self.PS[i], "ps%d" % i

    def mm(self, out, lhsT, rhs, start, stop, r, w, **kw):
        self.P.op("pe", lambda e: e.matmul(out, lhsT=lhsT, rhs=rhs, start=start, stop=stop, **kw), reads=r, writes=w)

    def tr(self, out, in_, ident, r, w):
        self.P.op("pe", lambda e: e.transpose(out=out, in_=in_, identity=ident), reads=r, writes=w)

    def act(self, out, in_, func, r, w, bias=None, scale=None, accum=None):
        kw = {}
        if bias is not None:
            kw["bias"] = bias
        if scale is not None:
            kw["scale"] = scale
        if accum is not None:
            kw["accum_out"] = accum
        self.P.op("act", lambda e: e.activation(out=out, in_=in_, func=func, **kw), reads=r, writes=w, par=self.par)

    def tt(self, eng, out, in0, in1, op, r, w):
        self.P.op(eng, lambda e: e.tensor_tensor(out=out, in0=in0, in1=in1, op=op), reads=r, writes=w)

    def ts(self, eng, out, in0, s1, s2, op0, op1, r, w):
        if s2 is None:
            self.P.op(eng, lambda e: e.tensor_scalar(out=out, in0=in0, scalar1=s1, scalar2=None, op0=op0), reads=r, writes=w, par=self.par)
        else:
            self.P.op(eng, lambda e: e.tensor_scalar(out=out, in0=in0, scalar1=s1, scalar2=s2, op0=op0, op1=op1), reads=r, writes=w)

    def stt(self, out, in0, scalar, in1, op0, op1, r, w):
        self.P.op("dve", lambda e: e.scalar_tensor_tensor(out=out, in0=in0, scalar=scalar, in1=in1, op0=op0, op1=op1),
                  reads=r, writes=w)

    def copy(self, eng, out, in_, r, w):
        if eng == "act":
            self.act(out, in_, AF.Copy, r, w)
        else:
            self.P.op(eng, lambda e: e.tensor_copy(out=out, in_=in_), reads=r, writes=w, par=self.par)

    def memset(self, eng, ap, val, w):
        self.P.op(eng, lambda e: e.memset(ap, val), writes=w)

    def recip(self, out, in_, r, w):
        self.P.op("dve", lambda e: e.reciprocal(out=out, in_=in_), reads=r, writes=w)

    def dma(self, eng, out, in_, r, w, is_out=False, par=False, **kw):
        self.P.op(eng, lambda e: e.dma_start(out=out, in_=in_, **kw), reads=r, writes=w, dma=True, is_out=is_out, par=par)

    def wload(self, dst, src2d, kc, w):
        for c in range(kc):
            self.dma("pool", dst[:, c, :], src2d[c * 128:(c + 1) * 128, :], [], w, par=(c > 0))

    def declare(self):
        nc, T, NSEQ = self.nc, self.T, self.NSEQ
        d = {}

        def inp(name, shape):
            d[name] = nc.dram_tensor(name, list(shape), F32, kind="ExternalInput").ap()

        inp("x", [NSEQ, T, D])
        inp("mem", [NSEQ, NMEM, D])
        inp("w_in", [D, DIN])
        inp("rw_w_up", [64, 512]); inp("rw_a_up", [64, 512]); inp("rw_g_up", [128, 512])
        inp("nsa_pe_k", [32, 64]); inp("nsa_pe_v", [32, 64])
        inp("nsa_ck1", [2048, 128]); inp("nsa_ck2", [128, 64]); inp("nsa_cv1", [2048, 128]); inp("nsa_cv2", [128, 64])
        inp("w_up_rw", [512, D]); inp("w_up_nsa", [512, D]); inp("w_out", [D, D])
        inp("xa_wq", [D, D]); inp("xa_wkv", [D, 2 * D]); inp("xa_wo", [D, D])
        inp("ffn_w_gu", [D, 2 * DFF]); inp("ffn_w_down", [DFF, D])
        inp("cols", [128, NCOLS])
        inp("c_identf", [128, 128]); inp("c_bdf", [128, 128])
        inp("c_bf", [128, 7 * 128])
        inp("c_rst", [128, 512])
        inp("c_cos", [128, T]); inp("c_sin", [128, T])
        inp("c_cmpmask", [128, T]); inp("c_eexp", [128, T])
        inp("c_selvalid", [128, T // 128, 32]); inp("c_seladd", [128, T // 128, 32])
        inp("c_ovl", [128, 32])
        if "inject" in self.dbg:
            inp("dbg_rw", [NSEQ, 512, T]); inp("dbg_nsa", [NSEQ, 512, T])
        if "inject_rw" in self.dbg:
            inp("dbg_rw", [NSEQ, 512, T])
        if "dump" in self.dbg:
            for nm in ("dbg_ons", "dbg_orw"):
                d[nm] = nc.dram_tensor(nm, [NSEQ, 512, T], F32, kind="ExternalOutput").ap()
        d["out"] = nc.dram_tensor("out", [NSEQ, T, D], F32, kind="ExternalOutput").ap()
        for name, shape in self.dbg.get("outs", {}).items():
            d[name] = nc.dram_tensor(name, list(shape), F32, kind="ExternalOutput").ap()
        self.d = d

    def setup_consts(self):
        A, d = self.A, self.d
        self.cols = A.alloc([NCOLS], F32)
        self.identf = A.alloc([128], F32)
        self.bdf = A.alloc([128], F32)
        self.cbf = A.alloc([7, 128], BF16)
        self.rst = A.alloc([512], F32)
        self.dma("sp", self.cols, d["cols"][:, :], [], ["cols"])
        self.dma("sp", self.identf, d["c_identf"][:, :], [], ["identf"])
        self.dma("sp", self.bdf, d["c_bdf"][:, :], [], ["bdf"])
        self.dma("sp", self.rst, d["c_rst"][:, :], [], ["rst"])
        self.dma("pool", self.cbf, d["c_bf"].rearrange("p (a b) -> p a b", a=7), [], ["cbf"])
        self.identb = self.cbf[:, 0, :]
        self.bdb = self.cbf[:, 1, :]
        self.onesb = self.cbf[:, 2, :]
        self.m_su = self.cbf[:, 3, :]
        self.m_sl = self.cbf[:, 4, :]
        self.m_iu = self.cbf[:, 5, :]
        self.permb = self.cbf[:, 6, :]
        self.CU = ["cols", "identf", "bdf", "cbf", "rst"]

    def col(self, base, c):
        return self.cols[:, base + c:base + c + 1]

    def tok_norm_T(self, src_rows, ntiles, gbase, dstT, dunit, rawT=None, rawunit=None, keep=(), nobar=False):
        A = self.A
        m = A.mark()
        xst = [A.alloc([D], F32) for _ in range(2)]
        if dstT is not None:
            xs = [A.alloc([D], F32) for _ in range(2)]
            junk = A.alloc([D], BF16)
            sm = A.alloc([8], F32)
        for tt in range(ntiles):
            b = tt % 2
            xu, su = "xst%d" % b, "xs%d" % b
            self.dma("sp", xst[b], src_rows(tt), [], [xu])
            if dstT is not None:
                self.act(junk, xst[b], AF.Square, [xu], ["junk", "sm"], accum=sm[:, 0:1])
                self.act(sm[:, 1:2], sm[:, 0:1], AF.Sqrt, ["sm"], ["sm"], bias=EPS, scale=1.0 / D)
                self.recip(sm[:, 2:3], sm[:, 1:2], ["sm"], ["sm"])
                self.act(xs[b], xst[b], AF.Copy, [xu, "sm"], [su], scale=sm[:, 2:3])
            for half in range(2):
                if dstT is not None:
                    bi, pb, pu = self.bank()
                    for j in range(4):
                        c = half * 4 + j
                        self.tr(pb[:, j * 128:(j + 1) * 128], xs[b][:, c * 128:(c + 1) * 128], self.identf, [su, "identf"], [pu])
                    for j in range(4):
                        c = half * 4 + j
                        if j % 2 == 0:
                            self.act(dstT[:, c, tt * 128:(tt + 1) * 128], pb[:, j * 128:(j + 1) * 128], AF.Copy,
                                     [pu, "cols"], [dunit], scale=self.col(gbase, c))
                        else:
                            self.ts("dve", dstT[:, c, tt * 128:(tt + 1) * 128], pb[:, j * 128:(j + 1) * 128],
                                    self.col(gbase, c), None, ALU.mult, None, [pu, "cols"], [dunit])
                if rawT is not None:
                    bi, pb, pu = self.bank()
                    for j in range(4):
                        c = half * 4 + j
                        self.tr(pb[:, j * 128:(j + 1) * 128], xst[b][:, c * 128:(c + 1) * 128], self.identf, [xu, "identf"], [pu])
                    self.copy("act" if half == 0 else "dve", rawT[:, half * 4:half * 4 + 4, tt * 128:(tt + 1) * 128],
                              pb.rearrange("p (a b) -> p a b", a=4), [pu], [rawunit])
        if not nobar:
            self.P.barrier(keep=keep)
            A.release(m)

    def norm_all(self, gbase, dstT=None, dunit="xnT", fp32_out=None, keep=()):
        A, T = self.A, self.T
        m = A.mark()
        sq = A.alloc([8, 512], BF16)
        rstd = A.alloc([512], F32)
        for blk in range(T // 512):
            sl = slice(blk * 512, (blk + 1) * 512)
            for c in range(8):
                if c % 2 == 0:
                    self.act(sq[:, c, :], self.hT[:, c, sl], AF.Square, ["hT%d" % blk], ["nsq"])
                else:
                    self.tt("pool", sq[:, c, :], self.hT[:, c, sl], self.hT[:, c, sl], ALU.mult, ["hT%d" % blk], ["nsq"])
            bi, pb, pu = self.bank()
            for c in range(8):
                self.mm(pb, self.onesb, sq[:, c, :], c == 0, c == 7, ["nsq", "cbf"], [pu])
            self.act(rstd, pb, AF.Sqrt, [pu], ["nrstd"], bias=EPS, scale=1.0 / D)
            self.recip(rstd, rstd, ["nrstd"], ["nrstd"])
            for c in range(8):
                o = dstT[:, c, sl] if fp32_out is None else fp32_out(blk, c)
                self.stt(o, self.hT[:, c, sl], self.col(gbase, c), rstd, ALU.mult, ALU.mult,
                         ["hT%d" % blk, "cols", "nrstd"], [dunit if fp32_out is None else "fo"])
            if fp32_out is not None:
                yield blk
        self.P.barrier(keep=keep)
        A.release(m)

    def merge1(self, nobar=False):
        A, T, d = self.A, self.T, self.d
        m = A.mark()
        wg1 = A.alloc([8, 512], BF16)
        wg2 = A.alloc([8, 512], BF16)
        wu1 = A.alloc([4, 512], BF16)
        wu2 = A.alloc([4, 512], BF16)
        g1 = A.alloc([512], BF16)
        g2 = A.alloc([512], BF16)
        ta = A.alloc([512], F32)
        tb = A.alloc([512], F32)
        for nh in range(2):
            self.wload(wg1, d["w_in"][:, C_GM + nh * 512:C_GM + nh * 512 + 512], 8, ["wg1"])
            self.wload(wg2, d["w_in"][:, C_GM + 1024 + nh * 512:C_GM + 1024 + nh * 512 + 512], 8, ["wg2"])
            self.wload(wu1, d["w_up_rw"][:, nh * 512:(nh + 1) * 512], 4, ["wu1"])
            self.wload(wu2, d["w_up_nsa"][:, nh * 512:(nh + 1) * 512], 4, ["wu2"])
            for blk in range(T // 512):
                sl = slice(blk * 512, (blk + 1) * 512)
                for nt in range(4):
                    ntile = nh * 4 + nt
                    ns = slice(nt * 128, (nt + 1) * 128)
                    _, p1, u1 = self.bank()
                    for c in range(8):
                        self.mm(p1, wg1[:, c, ns], self.xnT[:, c, sl], c == 0, c == 7, ["wg1", "xnT"], [u1])
                    self.act(g1, p1, AF.Sigmoid, [u1], ["g1"])
                    _, p2, u2 = self.bank()
                    for c in range(4):
                        self.mm(p2, wu1[:, c, ns], self.orwT[:, c, sl], c == 0, c == 3, ["wu1", "orwT"], [u2])
                    self.tt("dve", ta, g1, p2, ALU.mult, ["g1", u2], ["ta"])
                    _, p3, u3 = self.bank()
                    for c in range(8):
                        self.mm(p3, wg2[:, c, ns], self.xnT[:, c, sl], c == 0, c == 7, ["wg2", "xnT"], [u3])
                    self.act(g2, p3, AF.Sigmoid, [u3], ["g2"])
                    _, p4, u4 = self.bank()
                    for c in range(4):
                        self.mm(p4, wu2[:, c, ns], self.onsT[:, c, sl], c == 0, c == 3, ["wu2", "onsT"], [u4])
                    self.tt("dve", tb, g2, p4, ALU.mult, ["g2", u4], ["tb"])
                    self.tt("pool", self.mT[:, ntile, sl], ta, tb, ALU.add, ["ta", "tb"], ["mT"])
        if not nobar:
            self.P.barrier()
            A.release(m)

    def add_proj(self, W, wunit, kc, srcT, sunit, blk, sl):
        for ntile in range(8):
            _, pb, pu = self.bank()
            for c in range(kc):
                self.mm(pb, W[:, c, ntile * 128:(ntile + 1) * 128], srcT[:, c, sl], c == 0, c == kc - 1, [wunit, sunit], [pu])
            hs = self.hT[:, ntile, blk * 512:(blk + 1) * 512]
            self.tt("dve", hs, hs, pb, ALU.add, [pu, "hT%d" % blk], ["hT%d" % blk])

    def merge2(self, s):
        A, T, d = self.A, self.T, self.d
        m = A.mark()
        wo = A.alloc([8, D], BF16)
        self.wload(wo, d["w_out"], 8, ["wo"])
        for blk in range(T // 512):
            self.add_proj(wo, "wo", 8, self.mT, "mT", blk, slice(blk * 512, (blk + 1) * 512))
        self.P.barrier()
        A.release(m)

    def xattn(self, s):
        A, T, d = self.A, self.T, self.d
        m0 = A.mark()
        KT = A.alloc([8, NMEM], BF16)
        V = A.alloc([2, D], BF16)
        wq = A.alloc([8, D], BF16)
        wo = A.alloc([8, D], BF16)
        self.wload(wq, d["xa_wq"], 8, ["wq"])
        self.wload(wo, d["xa_wo"], 8, ["wox"])
        m1 = A.mark()
        memnT = A.alloc([8, NMEM], BF16)
        self.tok_norm_T(lambda tt: d["mem"][s, tt * 128:(tt + 1) * 128, :], NMEM // 128, CG_MEM, memnT, "memnT", keep=["wq", "wox"])
        wkv = A.alloc([8, 2 * D], BF16)
        self.wload(wkv[:, :, 0:D], d["xa_wkv"][:, 0:D], 8, ["wkvk"])
        self.wload(wkv[:, :, D:2 * D], d["xa_wkv"][:, D:2 * D], 8, ["wkvv"])
        for ntile in range(8):
            _, pb, pu = self.bank()
            for c in range(8):
                self.mm(pb[:, 0:NMEM], wkv[:, c, ntile * 128:(ntile + 1) * 128], memnT[:, c, :], c == 0, c == 7, ["wkvk", "memnT"], [pu])
            self.copy("act", KT[:, ntile, :], pb[:, 0:NMEM], [pu], ["KT"])
        for mt in range(2):
            for half in range(2):
                _, pb, pu = self.bank()
                for c in range(8):
                    self.mm(pb, memnT[:, c, mt * 128:(mt + 1) * 128], wkv[:, c, D + half * 512:D + (half + 1) * 512],
                            c == 0, c == 7, ["wkvv", "memnT"], [pu])
                self.copy("dve", V[:, mt, half * 512:(half + 1) * 512], pb, [pu], ["Vx"])
        self.P.barrier(keep=["wq", "wox"])
        A.release(m1)
        for _ in self.norm_all(CG_XA, self.xnT, keep=["wq", "wox"]):
            pass
        qx = A.alloc([8, 512], BF16)
        PTs = [A.alloc([2, 512], BF16) for _ in range(2)]
        rdens = [A.alloc([512], F32) for _ in range(2)]
        oT = A.alloc([8, 512], BF16)
        for blk in range(T // 512):
            sl = slice(blk * 512, (blk + 1) * 512)
            for ntile in range(8):
                _, pb, pu = self.bank()
                for c in range(8):
                    self.mm(pb, wq[:, c, ntile * 128:(ntile + 1) * 128], self.xnT[:, c, sl], c == 0, c == 7, ["wq", "xnT"], [pu])
                self.copy("act" if ntile % 2 == 0 else "dve", qx[:, ntile, :], pb, [pu], ["qx"])

            def xs(hx):
                PT, ptu_ = PTs[hx % 2], "PTx%d" % (hx % 2)
                for mt in range(2):
                    _, pb, pu = self.bank()
                    for dd in range(2):
                        self.mm(pb, KT[:, hx * 2 + dd, mt * 128:(mt + 1) * 128], qx[:, hx * 2 + dd, :], dd == 0, dd == 1, ["KT", "qx"], [pu])
                    self.act(PT[:, mt, :], pb, AF.Exp, [pu], [ptu_], scale=1.0 / 16.0)

            def xpv(hx):
                PT, ptu_ = PTs[hx % 2], "PTx%d" % (hx % 2)
                rden, ru_ = rdens[hx % 2], "rdenx%d" % (hx % 2)
                _, pb, pu = self.bank()
                for mt in range(2):
                    self.mm(pb, self.onesb, PT[:, mt, :], mt == 0, mt == 1, ["cbf", ptu_], [pu])
                self.recip(rden, pb, [pu], [ru_])
                for dd in range(2):
                    _, pb, pu = self.bank()
                    for mt in range(2):
                        self.mm(pb, V[:, mt, (hx * 2 + dd) * 128:(hx * 2 + dd + 1) * 128], PT[:, mt, :], mt == 0, mt == 1, ["Vx", ptu_], [pu])
                    self.tt("dve", oT[:, hx * 2 + dd, :], pb, rden, ALU.mult, [pu, ru_], ["oTx"])

            xs(0)
            for hx in range(4):
                if hx + 1 < 4:
                    xs(hx + 1)
                xpv(hx)
            self.add_proj(wo, "wox", 8, oT, "oTx", blk, slice(0, 512))
        self.P.barrier()
        A.release(m0)

    def ffn(self):
        A, T, d = self.A, self.T, self.d
        m = A.mark()
        QT = [6, 6, 5, 5]
        NFM = 6
        sets = []
        for i in range(2):
            sets.append((A.alloc([8, NFM * 128], BF16), A.alloc([8, NFM * 128], BF16), A.alloc([NFM, D], BF16)))
        actT = A.alloc([NFM, 512], BF16)
        sg = [A.alloc([512], F32) for _ in range(2)]

        def load(q):
            wg, wu, wd = sets[q % 2]
            f0 = sum(QT[:q]) * 128
            n = QT[q] * 128
            su = "ws%d" % (q % 2)
            self.wload(wg[:, :, 0:n], d["ffn_w_gu"][:, f0:f0 + n], 8, [su])
            self.wload(wu[:, :, 0:n], d["ffn_w_gu"][:, DFF + f0:DFF + f0 + n], 8, [su])
            self.wload(wd[:, 0:QT[q], :], d["ffn_w_down"][f0:f0 + n, :], QT[q], [su])

        load(0)
        load(1)
        for _ in self.norm_all(CG_FFN, self.xnT, keep=["ws0", "ws1"]):
            pass
        for q in range(4):
            wg, wu, wd = sets[q % 2]
            su = "ws%d" % (q % 2)
            NF = QT[q]
            for blk in range(T // 512):
                sl = slice(blk * 512, (blk + 1) * 512)
                for f in range(NF):
                    fs = slice(f * 128, (f + 1) * 128)
                    _, p1, u1 = self.bank()
                    for c in range(8):
                        self.mm(p1, wg[:, c, fs], self.xnT[:, c, sl], c == 0, c == 7, [su, "xnT"], [u1])
                    _, p2, u2 = self.bank()
                    for c in range(8):
                        self.mm(p2, wu[:, c, fs], self.xnT[:, c, sl], c == 0, c == 7, [su, "xnT"], [u2])
                    self.act(sg[f % 2], p1, AF.Silu, [u1], ["sg%d" % (f % 2)])
                    self.tt("dve", actT[:, f, :], sg[f % 2], p2, ALU.mult, ["sg%d" % (f % 2), u2], ["actT"])
                self.add_proj(wd, su, NF, actT, "actT", blk, slice(0, 512))
            if q + 2 < 4:
                load(q + 2)
        self.P.barrier()
        A.release(m)

    def final_out(self, s):
        A, T, d, P = self.A, self.T, self.d, self.P
        m = A.mark()
        yfs = [A.alloc([8, 512], F32) for _ in range(2)]
        ost = [A.alloc([D], F32) for _ in range(2)]
        sq = A.alloc([8, 512], BF16)
        rstd = A.alloc([512], F32)
        NBK = T // 512

        def norm_blk(blk):
            yf, yu = yfs[blk % 2], "fo%d" % (blk % 2)
            sl = slice(blk * 512, (blk + 1) * 512)
            for c in range(8):
                if c % 2 == 0:
                    self.act(sq[:, c, :], self.hT[:, c, sl], AF.Square, ["hT%d" % blk], ["nsq"])
                else:
                    self.tt("pool", sq[:, c, :], self.hT[:, c, sl], self.hT[:, c, sl], ALU.mult, ["hT%d" % blk], ["nsq"])
            _, pb, pu = self.bank()
            for c in range(8):
                self.mm(pb, self.onesb, sq[:, c, :], c == 0, c == 7, ["nsq", "cbf"], [pu])
            self.act(rstd, pb, AF.Sqrt, [pu], ["nrstd"], bias=EPS, scale=1.0 / D)
            self.recip(rstd, rstd, ["nrstd"], ["nrstd"])
            for c in range(8):
                self.stt(yf[:, c, :], self.hT[:, c, sl], self.col(CG_FIN, c), rstd, ALU.mult, ALU.mult,
                         ["hT%d" % blk, "cols", "nrstd"], [yu])

        kk = [0]

        def out_blk(blk):
            yf, yu = yfs[blk % 2], "fo%d" % (blk % 2)
            for t4 in range(4):
                b = kk[0] % 2
                kk[0] += 1
                for half in range(2):
                    _, pb, pu = self.bank()
                    for j in range(4):
                        c = half * 4 + j
                        self.tr(pb[:, j * 128:(j + 1) * 128], yf[:, c, t4 * 128:(t4 + 1) * 128], self.identf, [yu, "identf"], [pu])
                    self.copy("act" if half == 0 else "dve", ost[b][:, half * 512:(half + 1) * 512], pb, [pu], ["ost%d" % b])
                t0 = blk * 512 + t4 * 128
                self.dma("sp", d["out"][s, t0:t0 + 128, :], ost[b], ["ost%d" % b], [], is_out=True)

        def mergeo(a_, b_):
            out = []
            ia = ib = 0
            na, nb = len(a_), len(b_)
            while ia < na or ib < nb:
                fa = ia / na if ia < na else 2.0
                fb = ib / nb if ib < nb else 2.0
                if fa <= fb:
                    out.append(a_[ia]); ia += 1
                else:
                    out.append(b_[ib]); ib += 1
            return out

        self.bank_set = [6, 7]
        norm_blk(0)
        for blk in range(NBK):
            self.bank_set = [0, 1, 2, 3, 4, 5]
            oa = P.capture(lambda: out_blk(blk))
            ob = []
            if blk + 1 < NBK:
                self.bank_set = [6, 7]
                ob = P.capture(lambda: norm_blk(blk + 1))
            P.replay(mergeo(oa, ob))
        self.bank_set = None
        A.release(m)

    def build(self):
        A, T, d = self.A, self.T, None
        self.declare()
        d = self.d
        self.setup_consts()
        base = A.mark()
        for s in range(self.NSEQ):
            A.release(base)
            A.top = A.n
            self.xnTz = A.alloc([8, T + 32], BF16)
            self.xnT = self.xnTz[:, :, 32:T + 32]
            self.memset("pool", self.xnTz[:, :, 0:32], 0.0, ["xnTz0"])
            self.orwT = A.alloc([4, T], BF16, top=True)
            self.tok_norm_T(lambda tt: d["x"][s, tt * 128:(tt + 1) * 128, :], T // 128, CG_MIX, self.xnT, "xnT")
            if "inject" in self.dbg:
                self.onsT = A.alloc([4, T], BF16, top=True)
                self.dma("pool", self.orwT, d["dbg_rw"][s].rearrange("(c p) t -> p c t", p=128), [], ["orwT"])
                self.dma("pool", self.onsT, d["dbg_nsa"][s].rearrange("(c p) t -> p c t", p=128), [], ["onsT"])
            else:
                if "inject_rw" in self.dbg:
                    self.dma("pool", self.orwT, d["dbg_rw"][s].rearrange("(c p) t -> p c t", p=128), [], ["orwT"])
                else:
                    self.rwkv(s)
                self.onsT = A.alloc([4, T], BF16, top=True)
                self.nsa(s)
            if "dump" in self.dbg:
                self.dma("pool", d["dbg_ons"][s].rearrange("(c p) t -> p c t", p=128), self.onsT, ["onsT"], [], is_out=True)
                self.dma("pool", d["dbg_orw"][s].rearrange("(c p) t -> p c t", p=128), self.orwT, ["orwT"], [], is_out=True)
            self.mT = A.alloc([8, T], BF16, top=True)
            self.hT = A.alloc([8, T], F32)
            mk_ = A.mark()
            if self.dbg.get("ovl_reload", True):
                self.bank_set = [0, 1, 2, 3, 4, 5]
                o_m = self.P.capture(lambda: self.merge1(nobar=True))
                self.bank_set = [6, 7]
                o_r = self.P.capture(lambda: self.tok_norm_T(lambda tt: d["x"][s, tt * 128:(tt + 1) * 128, :], T // 128, 0, None, None,
                                                             rawT=self.hT, rawunit="hTall", nobar=True))
                self.bank_set = None
                mg = []
                ia = ib = 0
                while ia < len(o_m) or ib < len(o_r):
                    fa = ia / len(o_m) if ia < len(o_m) else 2.0
                    fb = ib / len(o_r) if ib < len(o_r) else 2.0
                    if fa <= fb:
                        mg.append(o_m[ia]); ia += 1
                    else:
                        mg.append(o_r[ib]); ib += 1
                self.P.replay(mg)
                self.P.barrier()
                A.release(mk_)
            else:
                self.merge1()
                self.tok_norm_T(lambda tt: d["x"][s, tt * 128:(tt + 1) * 128, :], T // 128, 0, None, None,
                                rawT=self.hT, rawunit="hTall")
            self.merge2(s)
            A.top = A.n
            self.xattn(s)
            self.ffn()
            self.final_out(s)
            self.P.barrier()
        self.P.finish()
        self.P.emit()


    def rwkv(self, s):
        A, T, d, P = self.A, self.T, self.d, self.P
        TB = 256
        NCK = TB // 128
        C0 = math.exp(-0.5)
        self.nrr = 6
        m0 = A.mark()
        wrw = A.alloc([8, 1792], BF16)
        self.wload(wrw, d["w_in"][:, 0:1792], 8, ["wrw"])
        lw = A.alloc([512], BF16)
        wlg = A.alloc([512], BF16)
        self.dma("pool", lw[0:64, :], d["rw_w_up"][:, :], [], ["lw"])
        self.dma("pool", lw[64:128, :], d["rw_a_up"][:, :], [], ["lw"])
        self.dma("pool", wlg, d["rw_g_up"][:, :], [], ["wlg"])
        Hf = A.alloc([4, 128], F32)
        Hb = A.alloc([4, 128], BF16)
        self.memset("pool", Hf, 0.0, ["Hf"])
        self.memset("pool", Hb, 0.0, ["Hb"])
        xwxa = A.alloc([TB], F32)
        xg = A.alloc([TB], F32)
        lx = A.alloc([TB], BF16)
        sxg = A.alloc([TB], BF16)
        rT = A.alloc([TB], F32)
        kT = A.alloc([TB], F32)
        vT = A.alloc([TB], F32)
        f = {nm: A.alloc([TB], F32) for nm in ("sg", "eta", "cs", "csx", "igam", "gamx", "kkr", "rn", "kk", "t1", "kp", "bt")}
        sqb = A.alloc([TB], BF16)
        rkb = A.alloc([TB], BF16)
        vb = A.alloc([4, TB], BF16)
        SH = []
        for _par in range(2):
            SH.append(dict(gam=A.alloc([4, TB], F32), AT=A.alloc([4, TB], BF16), RT=A.alloc([4, TB], BF16),
                           BT=A.alloc([4, TB], BF16), KTt=A.alloc([4, TB], BF16), gT=A.alloc([4, TB], BF16),
                           bonT=A.alloc([4, TB], BF16), tokV=A.alloc([NCK, 4, 128], BF16),
                           tokB=A.alloc([NCK, 4, 128], BF16), tokK=A.alloc([NCK, 4, 128], BF16)))
        Pm = [[[A.alloc([4, 128], BF16) for _ in range(2)] for _ in range(2)] for _ in range(NCK)]
        PTm = [[[A.alloc([4, 128], BF16) for _ in range(2)] for _ in range(2)] for _ in range(NCK)]
        Xm = [[[A.alloc([4, 128], BF16) for _ in range(2)] for _ in range(2)] for _ in range(NCK)]
        Lak = [[A.alloc([4, 128], BF16) for _ in range(2)] for _ in range(NCK)]
        Mrb = [[A.alloc([4, 128], BF16) for _ in range(2)] for _ in range(NCK)]
        Mrk = [[A.alloc([4, 128], BF16) for _ in range(2)] for _ in range(NCK)]
        W1s = A.alloc([512], BF16)
        Us = A.alloc([512], BF16)
        Ht = A.alloc([4, 128], F32)
        ysq = A.alloc([512], F32)
        yc = A.alloc([512], F32)
        ynb = A.alloc([512], BF16)
        st = A.alloc([64], F32)
        zt = A.alloc([128], F32)

        def bc4(ap128):
            return ap128.unsqueeze(1).to_broadcast([128, 4, 128])

        def v4(ap512):
            return ap512.rearrange("p (h c) -> p h c", h=4)

        carry = A.alloc([14], F32)
        self.memset("pool", carry, 0.0, ["carry"])
        praw = [A.alloc([TB + 1], F32) for _ in range(2)]
        dtmp = A.alloc([TB], F32)
        pr = [0]

        def proj_shift(ct, tok0, dst, dunit):
            b = pr[0] % 2
            pr[0] += 1
            pw, pwu = praw[b], "praw%d" % b
            _, pb, pu = self.bank()
            for c in range(8):
                self.mm(pb[:, 0:TB], wrw[:, c, ct * 128:(ct + 1) * 128], self.xnT[:, c, tok0:tok0 + TB], c == 0, c == 7, ["wrw", "xnT"], [pu])
            self.copy("pool", pw[:, 0:1], carry[:, ct:ct + 1], ["carry"], [pwu])
            self.copy("act", pw[:, 1:TB + 1], pb[:, 0:TB], [pu], [pwu])
            self.copy("pool", carry[:, ct:ct + 1], pw[:, TB:TB + 1], [pwu], ["carry"])
            self.tt("pool", dtmp, pw[:, 0:TB], pw[:, 1:TB + 1], ALU.subtract, [pwu], ["dtmp"])
            self.stt(dst, dtmp, self.col(C_MU, ct), pw[:, 1:TB + 1], ALU.mult, ALU.add, ["dtmp", pwu, "cols"], [dunit])

        SHK = ("gam", "AT", "RT", "BT", "KTt", "gT", "bonT", "tokV", "tokB", "tokK")

        def front(blk):
            par = blk % 2
            tok0 = blk * TB
            gam, AT, RT, BT, KTt, gT, bonT, tokV, tokB, tokK = (SH[par][k_] for k_ in SHK)
            ugam, uAT, uRT, uBT, uKTt, ugT, ubonT, utokV, utokB, utokK = ("%s%d" % (k_, par) for k_ in SHK)
            proj_shift(12, tok0, xwxa, "xwxa")
            proj_shift(13, tok0, xg, "xg")
            self.act(lx[0:64, :], xwxa[0:64, :], AF.Tanh, ["xwxa"], ["lx"])
            self.copy("pool", lx[64:128, :], xwxa[64:128, :], ["xwxa"], ["lx"])
            self.act(sxg, xg, AF.Sigmoid, ["xg"], ["sxg"])
            for hp in range(4):
                hs = slice(hp * 128, (hp + 1) * 128)
                proj_shift(hp, tok0, rT, "rT")
                proj_shift(4 + hp, tok0, kT, "kT")
                proj_shift(8 + hp, tok0, vT, "vT")
                _, pb, pu = self.bank()
                self.mm(pb[:, 0:TB], lw[0:64, hs], lx[0:64, :], True, True, ["lw", "lx"], [pu])
                self.act(f["sg"], pb[:, 0:TB], AF.Sigmoid, [pu, "cols"], ["sg"], bias=self.col(C_W0, hp))
                _, pb, pu = self.bank()
                self.mm(pb[:, 0:TB], lw[64:128, hs], lx[64:128, :], True, True, ["lw", "lx"], [pu])
                self.act(f["eta"], pb[:, 0:TB], AF.Sigmoid, [pu, "cols"], ["eta"], bias=self.col(C_A0, hp))
                _, pb, pu = self.bank()
                self.mm(pb[:, 0:TB], wlg[:, hs], sxg, True, True, ["wlg", "sxg"], [pu])
                self.copy("act", gT[:, hp, :], pb[:, 0:TB], [pu], [ugT])
                self.P.op("dve", lambda e: e.tensor_tensor_scan(out=f["cs"], data0=self.rst[:, 0:TB], data1=f["sg"], initial=0.0,
                                                                op0=ALU.mult, op1=ALU.add), reads=["sg", "rst"], writes=["cs"])
                self.act(gam[:, hp, :], f["cs"], AF.Exp, ["cs"], [ugam], scale=-C0)
                self.act(f["igam"], f["cs"], AF.Exp, ["cs"], ["igam"], scale=C0)
                self.tt("pool", f["csx"], f["cs"], f["sg"], ALU.subtract, ["cs", "sg"], ["csx"])
                self.act(f["gamx"], f["csx"], AF.Exp, ["csx"], ["gamx"], scale=-C0)
                self.ts("pool", f["kkr"], kT, self.col(C_KK, hp), None, ALU.mult, None, ["kT", "cols"], ["kkr"])
                self.act(sqb, f["kkr"], AF.Square, ["kkr"], ["sqb"])
                _, pb, pu = self.bank()
                self.mm(pb[:, 0:TB], self.bdb, sqb, True, True, ["cbf", "sqb"], [pu])
                self.ts("dve", f["rn"], pb[:, 0:TB], 1e-12, None, ALU.max, None, [pu], ["rn"])
                self.act(f["rn"], f["rn"], AF.Sqrt, ["rn"], ["rn"])
                self.recip(f["rn"], f["rn"], ["rn"], ["rn"])
                self.tt("pool", f["kk"], f["kkr"], f["rn"], ALU.mult, ["kkr", "rn"], ["kk"])
                self.ts("dve", f["t1"], f["eta"], -1.0, self.col(C_KA, hp), ALU.add, ALU.mult, ["eta", "cols"], ["t1"])
                self.stt(f["kp"], f["t1"], 1.0, kT, ALU.add, ALU.mult, ["t1", "kT"], ["kp"])
                self.stt(AT[:, hp, :], f["kk"], -1.0, f["gamx"], ALU.mult, ALU.mult, ["kk", "gamx"], [uAT])
                self.tt("pool", RT[:, hp, :], rT, gam[:, hp, :], ALU.mult, ["rT", ugam], [uRT])
                self.tt("pool", f["bt"], f["kk"], f["eta"], ALU.mult, ["kk", "eta"], ["bt"])
                self.tt("pool", BT[:, hp, :], f["bt"], f["igam"], ALU.mult, ["bt", "igam"], [uBT])
                self.tt("pool", KTt[:, hp, :], f["kp"], f["igam"], ALU.mult, ["kp", "igam"], [uKTt])
                self.copy("act", vb[:, hp, :], vT, ["vT"], ["vb"])
                self.stt(rkb, rT, self.col(C_RK, hp), f["kp"], ALU.mult, ALU.mult, ["rT", "kp", "cols"], ["rkb"])
                _, pb, pu = self.bank()
                self.mm(pb[:, 0:TB], self.bdb, rkb, True, True, ["cbf", "rkb"], [pu])
                self.tt("dve", bonT[:, hp, :], pb[:, 0:TB], vT, ALU.mult, [pu, "vT"], [ubonT])
            for cc in range(NCK):
                cs_ = slice(cc * 128, (cc + 1) * 128)
                for src, su, dst, du in ((vb, "vb", tokV, utokV), (BT, uBT, tokB, utokB), (KTt, uKTt, tokK, utokK)):
                    _, pb, pu = self.bank()
                    pbb = pb.bitcast(BF16)
                    for hp in range(4):
                        self.tr(pbb[:, hp * 128:(hp + 1) * 128], src[:, hp, cs_], self.identb, [su, "cbf"], [pu])
                    self.copy("act" if du != utokB else "dve", dst[:, cc, :, :], pbb[:, 0:512].rearrange("p (a b) -> p a b", a=4), [pu], [du])
        def back(blk, ccs, do_inv, do_chain):
            par = blk % 2
            tok0 = blk * TB
            gam, AT, RT, BT, KTt, gT, bonT, tokV, tokB, tokK = (SH[par][k_] for k_ in SHK)
            ugam, uAT, uRT, uBT, uKTt, ugT, ubonT, utokV, utokB, utokK = ("%s%d" % (k_, par) for k_ in SHK)
            chains = [(cc, e) for cc in ccs for e in range(2)] if do_inv else []
            for cc, e in chains:
                cs_ = slice(cc * 128, (cc + 1) * 128)
                es = slice(e * 64, (e + 1) * 64)
                cu_ = "c%de%d" % (cc, e)

                def score(lh, lu, rh, ru):
                    _, pb, pu = self.bank()
                    for hp in range(4):
                        self.mm(pb[:, hp * 128:(hp + 1) * 128], lh[es, hp, cs_], rh[es, hp, cs_], True, True, [lu, ru], [pu])
                    return pb, pu

                P0, PT0, X0 = Pm[cc][e][0], PTm[cc][e][0], Xm[cc][e][0]
                pb, pu = score(BT, uBT, AT, uAT)
                self.copy("act", P0, v4(pb), [pu], ["P0" + cu_])
                self.tt("pool", P0, P0, bc4(self.m_su), ALU.mult, ["P0" + cu_, "cbf"], ["P0" + cu_])
                pb, pu = score(AT, uAT, BT, uBT)
                self.copy("act", PT0, v4(pb), [pu], ["PT0" + cu_])
                self.tt("pool", PT0, PT0, bc4(self.m_sl), ALU.mult, ["PT0" + cu_, "cbf"], ["PT0" + cu_])
                pb, pu = score(KTt, uKTt, AT, uAT)
                self.copy("act", Lak[cc][e], v4(pb), [pu], ["Lak" + cu_])
                self.tt("pool", Lak[cc][e], Lak[cc][e], bc4(self.m_su), ALU.mult, ["Lak" + cu_, "cbf"], ["Lak" + cu_])
                pb, pu = score(BT, uBT, RT, uRT)
                self.tt("dve", Mrb[cc][e], v4(pb), bc4(self.m_iu), ALU.mult, [pu, "cbf"], ["Mrb" + cu_])
                pb, pu = score(KTt, uKTt, RT, uRT)
                self.tt("dve", Mrk[cc][e], v4(pb), bc4(self.m_iu), ALU.mult, [pu, "cbf"], ["Mrk" + cu_])
                self.tt("pool", X0, P0, bc4(self.identb), ALU.add, ["P0" + cu_, "cbf"], ["X0" + cu_])
            cur = 0
            for lvl in (range(1, 7) if do_inv else []):
                nxt = 1 - cur
                for cc, e in chains:
                    cu_ = "c%de%d" % (cc, e)
                    Pc_, PTc_, Xc_ = Pm[cc][e][cur], PTm[cc][e][cur], Xm[cc][e][cur]
                    Pn_, PTn_, Xn_ = Pm[cc][e][nxt], PTm[cc][e][nxt], Xm[cc][e][nxt]
                    cu, nu = "%d%s" % (cur, cu_), "%d%s" % (nxt, cu_)
                    _, pb, pu = self.bank()
                    for hp in range(4):
                        self.mm(pb[:, hp * 128:(hp + 1) * 128], Pc_[:, hp, :], PTc_[:, hp, :], True, True, ["P" + cu, "PT" + cu], [pu])
                    self.copy("act", PTn_, v4(pb), [pu], ["PT" + nu])
                    if lvl < 6:
                        _, pb, pu = self.bank()
                        for hp in range(4):
                            self.mm(pb[:, hp * 128:(hp + 1) * 128], PTc_[:, hp, :], Pc_[:, hp, :], True, True, ["P" + cu, "PT" + cu], [pu])
                        self.copy("act" if (cc + e) % 2 else "dve", Pn_, v4(pb), [pu], ["P" + nu])
                for cc, e in chains:
                    cu_ = "c%de%d" % (cc, e)
                    Xc_, Xn_, PTn_ = Xm[cc][e][cur], Xm[cc][e][nxt], PTm[cc][e][nxt]
                    cu, nu = "%d%s" % (cur, cu_), "%d%s" % (nxt, cu_)
                    _, pb, pu = self.bank()
                    for hp in range(4):
                        self.mm(pb[:, hp * 128:(hp + 1) * 128], PTn_[:, hp, :], Xc_[:, hp, :], True, True, ["PT" + nu, "X" + cu], [pu])
                    self.tt("dve", Xn_, v4(pb), Xc_, ALU.add, [pu, "X" + cu], ["X" + nu])
                cur = nxt
            assert cur == 0
            for cc in (ccs if do_chain else []):
                cs_ = slice(cc * 128, (cc + 1) * 128)
                gtok = tok0 + cc * 128
                _, pw1, w1u = self.acc_bank()
                for hp in range(4):
                    self.mm(pw1[:, hp * 128:(hp + 1) * 128], AT[:, hp, cs_], Hb[:, hp, :], hp == 0, False, [uAT, "Hb"], [w1u], skip_group_check=True)
                _, py, yu = self.acc_bank()
                for hp in range(4):
                    self.mm(py[:, hp * 128:(hp + 1) * 128], RT[:, hp, cs_], Hb[:, hp, :], hp == 0, False, [uRT, "Hb"], [yu], skip_group_check=True)
                for hp in range(4):
                    for e in range(2):
                        o = hp * 128 + e * 64
                        self.mm(pw1[:, o:o + 64], Lak[cc][e][:, hp, :], tokV[:, cc, hp, e * 64:(e + 1) * 64], False, hp == 3 and e == 1,
                                ["Lakc%de%d" % (cc, e), utokV], [w1u], skip_group_check=True)
                self.copy("act", W1s, pw1, [w1u], ["W1s"])
                _, pu_, uu = self.bank()
                for hp in range(4):
                    for e in range(2):
                        o = hp * 128 + e * 64
                        self.mm(pu_[:, o:o + 64], Xm[cc][e][0][:, hp, :], W1s[:, o:o + 64], True, True, ["X0c%de%d" % (cc, e), "W1s"], [uu])
                self.copy("act", Us, pu_, [uu], ["Us"])
                for hp in range(4):
                    for e in range(2):
                        o = hp * 128 + e * 64
                        self.mm(py[:, o:o + 64], Mrb[cc][e][:, hp, :], Us[:, o:o + 64], False, False, ["Mrbc%de%d" % (cc, e), "Us"], [yu], skip_group_check=True)
                        self.mm(py[:, o:o + 64], Mrk[cc][e][:, hp, :], tokV[:, cc, hp, e * 64:(e + 1) * 64], False, hp == 3 and e == 1,
                                ["Mrkc%de%d" % (cc, e), utokV], [yu], skip_group_check=True)
                _, pd, du_ = self.bank()
                for hp in range(4):
                    hs = slice(hp * 128, (hp + 1) * 128)
                    self.mm(pd[:, hs], tokB[:, cc, hp, :], Us[:, hs], True, False, [utokB, "Us"], [du_])
                    self.mm(pd[:, hs], tokK[:, cc, hp, :], tokV[:, cc, hp, :], False, True, [utokK, utokV], [du_])
                self.tt("dve", Ht, v4(pd), Hf, ALU.add, [du_, "Hf"], ["Ht"])
                for hp in range(4):
                    self.stt(Hf[:, hp, :], Ht[:, hp, :], gam[:, hp, cc * 128 + 127:cc * 128 + 128], self.bdf, ALU.mult, ALU.mult,
                             ["Ht", ugam, "bdf"], ["Hf"])
                self.copy("act", Hb, Hf, ["Hf"], ["Hb"])
                y3 = py.rearrange("p (h c) -> p h c", h=8)
                s1, s2, mean, msq, var = (st[:, i * 8:(i + 1) * 8] for i in range(5))
                self.P.op("dve", lambda e, s1=s1, y3=y3: e.tensor_reduce(out=s1, in_=y3, axis=mybir.AxisListType.X, op=ALU.add),
                          reads=[yu], writes=["st"])
                self.act(ysq, py, AF.Square, [yu], ["ysq"])
                self.P.op("dve", lambda e, s2=s2: e.tensor_reduce(out=s2, in_=ysq.rearrange("p (h c) -> p h c", h=8),
                                                                  axis=mybir.AxisListType.X, op=ALU.add), reads=["ysq"], writes=["st"])
                self.ts("dve", mean, s1, 1.0 / 64, None, ALU.mult, None, ["st"], ["st"])
                self.tt("dve", msq, mean, mean, ALU.mult, ["st"], ["st"])
                self.stt(var, s2, 1.0 / 64, msq, ALU.mult, ALU.subtract, ["st"], ["st"])
                self.act(var, var, AF.Sqrt, ["st"], ["st"], bias=64e-5)
                self.recip(var, var, ["st"], ["st"])
                yc3 = yc.rearrange("p (h c) -> p h c", h=8)
                self.tt("dve", yc3, y3, mean.unsqueeze(2).to_broadcast([128, 8, 64]), ALU.subtract, [yu, "st"], ["yc"])
                self.tt("pool", ynb.rearrange("p (h c) -> p h c", h=8), yc3, var.unsqueeze(2).to_broadcast([128, 8, 64]), ALU.mult,
                        ["yc", "st"], ["ynb"])
                _, pt, ptu = self.bank()
                ptb = pt.bitcast(BF16)
                for hp in range(4):
                    self.tr(ptb[:, hp * 128:(hp + 1) * 128], ynb[:, hp * 128:(hp + 1) * 128], self.identb, ["ynb", "cbf"], [ptu])
                for hp in range(4):
                    self.act(zt, ptb[:, hp * 128:(hp + 1) * 128], AF.Identity, [ptu, "cols"], ["zt"],
                             bias=self.col(C_LNB, hp), scale=self.col(C_LNG, hp))
                    self.tt("pool", zt, zt, bonT[:, hp, cs_], ALU.add, ["zt", ubonT], ["zt"])
                    self.tt("pool", self.orwT[:, hp, gtok:gtok + 128], zt, gT[:, hp, cs_], ALU.mult, ["zt", ugT], ["orwT"])
        def merge(a_, b_):
            out = []
            ia = ib = 0
            na, nb = len(a_), len(b_)
            while ia < na or ib < nb:
                fa = ia / na if ia < na else 2.0
                fb = ib / nb if ib < nb else 2.0
                if fa <= fb:
                    out.append(a_[ia]); ia += 1
                else:
                    out.append(b_[ib]); ib += 1
            return out

        NB = T // TB

        def cap(bs, fn):
            self.bank_set = bs
            return P.capture(fn)

        self.bank_set = [0, 1]
        front(0)
        self.bank_set = [2, 3, 4]
        back(0, [0], True, False)
        NC_ = NB * NCK
        for c in range(NC_):
            blk, cc = divmod(c, NCK)
            o_chain = cap([5], lambda: back(blk, [cc], False, True))
            o_bg = []
            if cc == 0 and blk + 1 < NB:
                o_bg = cap([0, 1], lambda: front(blk + 1))
            if c + 1 < NC_:
                blk2, cc2 = divmod(c + 1, NCK)
                o_inv = cap([2, 3, 4], lambda: back(blk2, [cc2], True, False))
                o_bg = merge(o_inv, o_bg)
            P.replay(merge(o_chain, o_bg))
        self.bank_set = None
        P.barrier()
        A.release(m0)
        self.nrr = 8


    def nsa(self, s):
        A, T, d, P = self.A, self.T, self.d, self.P
        NT = T // 128
        NCMP = (T - 32) // 16 + 1
        SC = 0.125
        self.nrr = 6
        win3 = d["w_in"].rearrange("(c p) n -> p c n", p=128)
        m0 = A.mark()
        qT = A.alloc([4, T], BF16)
        qrT = A.alloc([4, T], BF16)
        ks2 = A.alloc([2, 2, T], BF16)
        kw2 = A.alloc([2, 2, T], BF16)
        Vs = A.alloc([NT, 2, 65], BF16)
        Vw = A.alloc([NT, 2, 65], BF16)
        gates = A.alloc([NT, 24], F32)
        kc2 = A.alloc([2, 2, 128], BF16)
        Vc = A.alloc([2, 97], BF16)
        ovl = A.alloc([32], F32)
        self.dma("sp", ovl, d["c_ovl"][:, :], [], ["ovl"])
        self.memset("pool", ks2, 0.0, ["ks2"])
        self.memset("pool", kw2, 0.0, ["kw2"])
        self.memset("pool", kc2, 0.0, ["kc2"])
        self.memset("pool", Vs[:, :, :, 64:65], 1.0, ["Vs"])
        self.memset("pool", Vw[:, :, :, 64:65], 1.0, ["Vw"])
        self.memset("pool", Vc[:, :, 64:65], 1.0, ["Vc"])
        for g in range(2):
            self.copy("pool", Vc[:, g, 65:97], ovl, ["ovl"], ["Vc"])
        m1 = A.mark()
        kcT = A.alloc([T], BF16)
        vcT = A.alloc([T], BF16)
        m1b = A.mark()
        NW = 1304 + 512
        wn = A.alloc([8, NW], BF16)
        self.wload(wn[:, :, 0:1304], d["w_in"][:, C_Q:C_Q + 1304], 8, ["wn"])
        W_KC, W_VC, W_KS, W_VS, W_KW, W_VW, W_GN, W_KSD, W_KWD = 512, 640, 768, 896, 1024, 1152, 1280, 1304, 1560
        i_ = 0
        for g in range(2):
            for rep in range(2):
                for (dbase, sbase) in ((W_KSD, W_KS), (W_KWD, W_KW)):
                    self.copy(("act", "dve", "pool")[i_ % 3], wn[:, :, dbase + g * 128 + rep * 64:dbase + g * 128 + rep * 64 + 64],
                              wn[:, :, sbase + g * 64:sbase + g * 64 + 64], ["wn"], ["wnd"])
                    i_ += 1
        m2 = A.mark()
        cosT = A.alloc([512], F32)
        sinT = A.alloc([512], F32)
        kraw = A.alloc([512], BF16)
        t1 = A.alloc([512], F32)
        t2 = A.alloc([512], F32)

        def proj(col0, blk, out, ounit, eng="act"):
            sl = slice(blk * 512, (blk + 1) * 512)
            _, pb, pu = self.bank()
            for c in range(8):
                self.mm(pb, wn[:, c, col0:col0 + 128], self.xnT[:, c, sl], c == 0, c == 7, ["wn", "wnd", "xnT"], [pu])
            self.copy(eng, out, pb, [pu], [ounit])

        def rope(src, sunit, dst, dunit, blk, split=None):
            sl = slice(blk * 512, (blk + 1) * 512)
            _, pb, pu = self.bank()
            self.mm(pb, self.permb, src, True, True, ["cbf", sunit], [pu])
            self.tt("dve", t1, src, cosT, ALU.mult, [sunit, "rope"], ["t1"])
            self.tt("dve", t2, pb, sinT, ALU.mult, [pu, "rope"], ["t2"])
            if split is None:
                self.tt("pool", dst, t1, t2, ALU.add, ["t1", "t2"], [dunit])
            else:
                for e in range(2):
                    es = slice(e * 64, (e + 1) * 64)
                    self.tt("pool", split[es, e, sl], t1[es, :], t2[es, :], ALU.add, ["t1", "t2"], [dunit])

        for blk in range(T // 512):
            sl = slice(blk * 512, (blk + 1) * 512)
            self.dma("sp", cosT, d["c_cos"][:, sl], [], ["rope"])
            self.dma("sp", sinT, d["c_sin"][:, sl], [], ["rope"])
            for hp in range(4):
                proj(hp * 128, blk, qT[:, hp, sl], "qT")
                rope(qT[:, hp, sl], "qT", qrT[:, hp, sl], "qrT", blk)
            for g in range(2):
                proj(W_KSD + g * 128, blk, kraw, "kraw")
                rope(kraw, "kraw", None, "ks2", blk, split=ks2[:, g])
                proj(W_KWD + g * 128, blk, kraw, "kraw")
                rope(kraw, "kraw", None, "kw2", blk, split=kw2[:, g])
        P.barrier()
        A.release(m2)
        if self.dbg.get("nsa_stop", 99) <= 1:
            A.release(m0); self.nrr = 8; return
        vT = A.alloc([512], BF16)
        for (col0, Vd, vu) in ((W_VS, Vs, "Vs"), (W_VW, Vw, "Vw")):
            for blk in range(T // 512):
                proj(col0, blk, vT, "vT")
                _, pb, pu = self.bank()
                pbb = pb.bitcast(BF16)
                for j in range(4):
                    self.tr(pbb[:, j * 128:(j + 1) * 128], vT[:, j * 128:(j + 1) * 128], self.identb, ["vT", "cbf"], [pu])
                for j in range(4):
                    tt_ = blk * 4 + j
                    self.copy("act" if j % 2 == 0 else "dve", Vd[:, tt_, :, 0:64],
                              pbb[:, j * 128:(j + 1) * 128].rearrange("p (g c) -> p g c", g=2), [pu], [vu])
        for tt_ in range(NT):
            _, pb, pu = self.bank()
            for c in range(8):
                self.mm(pb[:, 0:24], self.xnT[:, c, tt_ * 128:(tt_ + 1) * 128], wn[:, c, W_GN:W_GN + 24], c == 0, c == 7, ["wn", "xnT"], [pu])
            self.act(gates[:, tt_, :], pb[:, 0:24], AF.Sigmoid, [pu], ["gates"])
        if self.dbg.get("nsa_stop", 99) <= 2:
            P.barrier(); A.release(m0); self.nrr = 8; return
        for blk in range(T // 512):
            sl = slice(blk * 512, (blk + 1) * 512)
            proj(W_KC, blk, kcT[:, sl], "kcT")
            proj(W_VC, blk, vcT[:, sl], "vcT", eng="dve")
        P.barrier()
        A.release(m1b)
        w1 = [A.alloc([32, 128], BF16) for _ in range(2)]
        w2k = A.alloc([128], BF16)
        w2v = A.alloc([64], BF16)
        peT = A.alloc([2, 32], BF16)
        bias = A.alloc([2], F32)
        hid = A.alloc([2, 128], BF16)
        gu = [A.alloc([128], F32) for _ in range(3)]
        for i, (nm, pen) in enumerate((("nsa_ck1", "nsa_pe_k"), ("nsa_cv1", "nsa_pe_v"))):
            src = d[nm].rearrange("(l dd) j -> dd l j", dd=64)
            for rep in range(2):
                self.dma("pool", w1[i][rep * 64:(rep + 1) * 64, :, :], src, [], ["w1_%d" % i])
                self.dma("pool", peT[rep * 64:(rep + 1) * 64, i, :], d[pen].rearrange("l dd -> dd l"), [], ["peT"],
                         allow_slow_non_contiguous=True)
        for rep in range(2):
            self.dma("pool", w2k[:, rep * 64:(rep + 1) * 64], d["nsa_ck2"][:, :], [], ["w2k"])
        self.dma("pool", w2v, d["nsa_cv2"][:, :], [], ["w2v"])
        self.memset("pool", hid, 0.0, ["hid"])
        for i in range(2):
            _, pb, pu = self.bank()
            for l in range(32):
                self.mm(pb[:, 0:1], w1[i][0:64, l, :], peT[0:64, i, l:l + 1], l == 0, l == 31, ["w1_%d" % i, "peT"], [pu])
            self.copy("dve", bias[:, i:i + 1], pb[:, 0:1], [pu], ["cbias"])
        for g in range(2):
            gs = slice(g * 64, (g + 1) * 64)
            for i, srcT in enumerate((kcT, vcT)):
                _, pb, pu = self.bank()
                for l in range(32):
                    self.mm(pb[:, 0:NCMP], w1[i][gs, l, :], srcT[gs, l:l + 16 * (NCMP - 1) + 1:16], l == 0, l == 31,
                            ["w1_%d" % i, "kcT", "vcT"], [pu])
                u, u2, u3 = gu[0][:, 0:NCMP], gu[1][:, 0:NCMP], gu[2][:, 0:NCMP]
                self.act(u, pb[:, 0:NCMP], AF.Identity, [pu, "cbias"], ["gu0"], bias=bias[:, i:i + 1])
                self.tt("pool", u2, u, u, ALU.mult, ["gu0"], ["gu1"])
                self.ts("dve", u2, u2, 0.044715, 1.0, ALU.mult, ALU.add, ["gu1"], ["gu1"])
                self.tt("pool", u3, u2, u, ALU.mult, ["gu1", "gu0"], ["gu2"])
                self.act(u3, u3, AF.Sigmoid, ["gu2"], ["gu2"], scale=1.5957691216)
                self.tt("dve", hid[:, i, 0:NCMP], u3, u, ALU.mult, ["gu2", "gu0"], ["hid"])
            _, pb, pu = self.bank()
            self.mm(pb[:, 0:128], w2k, hid[:, 0, :], True, True, ["w2k", "hid"], [pu])
            self.copy("act", kc2[0:64, g, 0, :], pb[0:64, 0:128], [pu], ["kc2"])
            self.copy("act", kc2[64:128, g, 1, :], pb[64:128, 0:128], [pu], ["kc2"])
            _, pb, pu = self.bank()
            self.mm(pb[:, 0:64], hid[:, 1, :], w2v, True, True, ["w2v", "hid"], [pu])
            self.copy("dve", Vc[:, g, 0:64], pb[:, 0:64], [pu], ["Vc"])
        P.barrier()
        A.release(m1)
        if self.dbg.get("nsa_stop", 99) <= 3:
            A.release(m0); self.nrr = 8; return
        selT = A.alloc([2, T], BF16)
        cmpm = A.alloc([T], BF16)
        eexp = A.alloc([T], BF16)
        selv = A.alloc([NT, 32], F32)
        sela = A.alloc([NT, 32], F32)
        self.dma("pool", cmpm, d["c_cmpmask"][:, :], [], ["cmpm"])
        self.dma("pool", eexp, d["c_eexp"][:, :], [], ["eexp"])
        self.memset("pool", selT, 0.0, ["selTz"])
        self.dma("sp", selv, d["c_selvalid"][:, :, :], [], ["selv"])
        self.dma("sp", sela, d["c_seladd"][:, :, :], [], ["sela"])
        NPB = 4
        Pb = [A.alloc([512], BF16) for _ in range(NPB)]
        Pcb = [A.alloc([512], BF16) for _ in range(2)]
        mk = A.alloc([128], BF16)
        oaccs = [A.alloc([8, 64], F32) for _ in range(2)]
        obf = A.alloc([512], BF16)
        sms = [A.alloc([64], F32) for _ in range(3)]
        impns = [A.alloc([4, 32], F32) for _ in range(2)]
        imps = [A.alloc([32], F32) for _ in range(2)]
        sel01s = [A.alloc([32], BF16) for _ in range(2)]
        otmp = A.alloc([4, 64], F32)
        pk = [0]

        def bc4(ap128):
            return ap128.unsqueeze(1).to_broadcast([128, 4, 128])

        def v4(ap512):
            return ap512.rearrange("p (h c) -> p h c", h=4)

        def scores(KT2, kunit, QT, qunit, kt, g, qs):
            _, pb, pu = self.bank()
            for hh in range(4):
                e = hh % 2
                pair = g * 2 + hh // 2
                self.mm(pb[:, hh * 128:(hh + 1) * 128], KT2[:, g, e, kt * 128:(kt + 1) * 128], QT[:, pair, qs],
                        True, True, [kunit, qunit], [pu])
            return pb, pu

        def post(bo, pu, b, first, qt, g, sm, smu):
            oacc = oaccs[qt % 2]
            ou_ = "oacc%d" % (qt % 2)
            gate = gates[:, qt, g * 12:(g + 1) * 12].rearrange("p (h b) -> p h b", b=3)[:, :, b:b + 1]
            rd = sm[:, 0:4].unsqueeze(2)
            cf = sm[:, 4:8].unsqueeze(2)
            self.ts("dve", rd, bo[:, :, 64:65], 1e-30, None, ALU.max, None, [pu], [smu])
            self.recip(rd, rd, [smu], [smu])
            self.tt("dve", cf, rd, gate, ALU.mult, [smu, "gates"], [smu])
            og = oacc[:, g * 4:(g + 1) * 4, :]
            if first:
                self.tt("dve", og, bo[:, :, 0:64], cf.to_broadcast([128, 4, 64]), ALU.mult, [pu, smu], [ou_])
            else:
                self.tt("dve", otmp, bo[:, :, 0:64], cf.to_broadcast([128, 4, 64]), ALU.mult, [pu, smu], ["otmp"])
                self.tt("pool", og, og, otmp, ALU.add, ["otmp", ou_], [ou_])
            return rd

        def cmp_part(qt, g):
            qs = slice(qt * 128, (qt + 1) * 128)
            sm, smu = sms[g], "sm%d" % g
            impn, imp, sel01 = impns[g], imps[g], sel01s[g]
            iu, nu, su = "imp%d" % g, "impn%d" % g, "sel01%d" % g
            Pc, pcu = Pcb[g], "Pcb%d" % g
            pb, pu = scores(kc2, "kc2", qT, "qT", 0, g, qs)
            self.act(Pc, pb, AF.Exp, [pu], [pcu], scale=SC)
            self.tt("pool", v4(Pc), v4(Pc), bc4(cmpm[:, qs]), ALU.mult, [pcu, "cmpm"], [pcu])
            _, po, ou = self.bank()
            for hh in range(4):
                self.mm(po[:, hh * 97:(hh + 1) * 97], Pc[:, hh * 128:(hh + 1) * 128], Vc[:, g, :], True, True, [pcu, "Vc"], [ou])
            bo = po[:, 0:388].rearrange("p (h c) -> p h c", h=4)
            rd = post(bo, ou, 0, True, qt, g, sm, smu)
            self.tt("dve", impn, bo[:, :, 65:97], rd.to_broadcast([128, 4, 32]), ALU.mult, [ou, smu], [nu])
            self.P.op("dve", lambda e: e.tensor_reduce(out=imp, in_=impn.rearrange("p h c -> p c h"),
                                                       axis=mybir.AxisListType.X, op=ALU.add),
                      reads=[nu], writes=[iu])
            self.tt("pool", imp, imp, selv[:, qt, :], ALU.mult, [iu, "selv"], [iu])
            self.tt("pool", imp, imp, sela[:, qt, :], ALU.add, [iu, "sela"], [iu])
            self.P.op("dve", lambda e: e.max(out=sm[:, 8:16], in_=imp), reads=[iu], writes=[smu])
            self.ts("dve", sel01, imp, sm[:, 15:16], None, ALU.is_ge, None, [iu, smu], [su])
            _, pt, ptu = self.bank()
            ptb = pt.bitcast(BF16)
            self.tr(ptb[0:32, 0:128], sel01, self.identb, [su, "cbf"], [ptu])
            self.copy("act", selT[0:32, g, qs], ptb[0:32, 0:128], [ptu, "selTz"], ["selT%d_%d" % (g, qt)])

        def selwin_part(qt, g):
            qs = slice(qt * 128, (qt + 1) * 128)
            sm, smu = sms[2], "sm2"
            stu = "selT%d_%d" % (g, qt)
            _, pacc_s, accu_s = self.acc_bank()
            _, pacc_w, accu_w = self.acc_bank()
            k0 = max(0, qt - 4)
            tasks = [("sel", kt) for kt in range(qt + 1)] + [("win", kt) for kt in range(k0, qt + 1)]

            def stage_s(task):
                kind, kt = task
                Pq = Pb[pk[0] % NPB]; pqu = "Pb%d" % (pk[0] % NPB); pk[0] += 1
                if kind == "sel":
                    pb, pu = scores(ks2, "ks2", qrT, "qrT", kt, g, qs)
                    _, pm, pmu = self.bank()
                    self.mm(pm[:, 0:128], eexp[:, kt * 128:(kt + 1) * 128], selT[:, g, qs], True, True, ["eexp", stu, "selTz"], [pmu])
                    self.act(Pq, pb, AF.Exp, [pu], [pqu], scale=SC)
                    if kt == qt:
                        self.tt("dve", mk, pm[:, 0:128], self.m_iu, ALU.mult, [pmu, "cbf"], ["mk"])
                        self.tt("pool", v4(Pq), v4(Pq), bc4(mk), ALU.mult, [pqu, "mk"], [pqu])
                    else:
                        self.tt("dve", v4(Pq), v4(Pq), bc4(pm[:, 0:128]), ALU.mult, [pqu, pmu], [pqu])
                else:
                    pb, pu = scores(kw2, "kw2", qrT, "qrT", kt, g, qs)
                    self.act(Pq, pb, AF.Exp, [pu], [pqu], scale=SC)
                    if kt == qt:
                        self.tt("pool", v4(Pq), v4(Pq), bc4(self.m_iu), ALU.mult, [pqu, "cbf"], [pqu])
                    elif kt == qt - 4:
                        self.tt("pool", v4(Pq), v4(Pq), bc4(self.m_sl), ALU.mult, [pqu, "cbf"], [pqu])
                return Pq, pqu

            def stage_pv(task, Pq, pqu):
                kind, kt = task
                if kind == "sel":
                    pacc, accu, Vd, vu, first, b = pacc_s, accu_s, Vs, "Vs", 0, 1
                else:
                    pacc, accu, Vd, vu, first, b = pacc_w, accu_w, Vw, "Vw", k0, 2
                for hh in range(4):
                    self.mm(pacc[:, hh * 65:(hh + 1) * 65], Pq[:, hh * 128:(hh + 1) * 128], Vd[:, kt, g, :],
                            kt == first and hh == 0, kt == qt and hh == 3, [pqu, vu], [accu], skip_group_check=True)
                if kt == qt:
                    post(pacc[:, 0:260].rearrange("p (h c) -> p h c", h=4), accu, b, False, qt, g, sm, smu)

            LA = 2
            inflight = []
            for task in tasks:
                inflight.append((task,) + stage_s(task))
                if len(inflight) > LA:
                    stage_pv(*inflight.pop(0))
            while inflight:
                stage_pv(*inflight.pop(0))

        def fin_part(qt):
            qs = slice(qt * 128, (qt + 1) * 128)
            oacc = oaccs[qt % 2]
            self.copy("act", obf, oacc.rearrange("p h c -> p (h c)"), ["oacc%d" % (qt % 2)], ["obf"])
            _, pt, ptu = self.bank()
            ptb = pt.bitcast(BF16)
            for j in range(4):
                self.tr(ptb[:, j * 128:(j + 1) * 128], obf[:, j * 128:(j + 1) * 128], self.identb, ["obf", "cbf"], [ptu])
            self.copy("dve", self.onsT[:, :, qs], ptb[:, 0:512].rearrange("p (a b) -> p a b", a=4), [ptu], ["onsT"])

        def merge2(a_, b_):
            out = []
            ia = ib = 0
            na, nb = len(a_), len(b_)
            while ia < na or ib < nb:
                fa = ia / na if ia < na else 2.0
                fb = ib / nb if ib < nb else 2.0
                if fa <= fb:
                    out.append(a_[ia]); ia += 1
                else:
                    out.append(b_[ib]); ib += 1
            return out

        def cap(bs, fn):
            self.bank_set = bs
            r_ = P.capture(fn)
            self.bank_set = None
            return r_

        def cmp_both(qt):
            return merge2(cap([4], lambda: cmp_part(qt, 0)), cap([5], lambda: cmp_part(qt, 1)))

        P.replay(cmp_both(0))
        for qt in range(NT):
            o_main = cap([0, 1, 2, 3], lambda: (selwin_part(qt, 0), selwin_part(qt, 1), fin_part(qt)))
            o_bg = cmp_both(qt + 1) if qt + 1 < NT else []
            P.replay(merge2(o_main, o_bg))
        P.barrier()
        A.release(m0)
        self.nrr = 8


def host_consts(T):
    c = {}
    c["c_identf"] = np.eye(128, dtype=np.float32)
    bd = np.zeros((128, 128), np.float32)
    bd[:64, :64] = 1
    bd[64:, 64:] = 1
    c["c_bdf"] = bd
    s_ = np.arange(128)[:, None]
    t_ = np.arange(128)[None, :]
    perm = np.zeros((128, 128), np.float32)
    for m in range(128):
        src = m + 32 if (m % 64) < 32 else m - 32
        perm[src, m] = 1
    c["c_bf"] = np.concatenate([np.eye(128), bd, np.ones((128, 128)), (s_ < t_), (s_ > t_), (s_ <= t_), perm], axis=1).astype(np.float32)
    rst = np.ones((128, 512), np.float32)
    rst[:, ::128] = 0
    c["c_rst"] = rst
    half = 32
    inv = (10000.0 ** (-np.arange(half, dtype=np.float32) / half)).astype(np.float32)
    ang = (np.arange(T, dtype=np.float32)[None, :] * inv[:, None]).astype(np.float32)
    cos = np.cos(ang).astype(np.float32)
    sin = np.sin(ang).astype(np.float32)
    cos64 = np.concatenate([cos, cos], 0)
    sin64 = np.concatenate([-sin, sin], 0)
    c["c_cos"] = np.concatenate([cos64, cos64], 0)
    c["c_sin"] = np.concatenate([sin64, sin64], 0)
    n = np.arange(128)[:, None]
    tq = np.arange(T)[None, :]
    cm = ((16 * n + 31) <= tq) & (n < (T - 32) // 16 + 1)
    c["c_cmpmask"] = cm.astype(np.float32)
    j = np.arange(32)[:, None]
    kpos = np.arange(T)[None, :]
    ee = np.zeros((128, T), np.float32)
    ee[:32] = (j == kpos // 64)
    c["c_eexp"] = ee
    tt = np.arange(T)
    cur = tt // 64
    jj = np.arange(32)[None, :]
    forced = (jj == 0) | (jj == cur[:, None]) | (jj == cur[:, None] - 1)
    valid = (jj <= cur[:, None]) & ~forced
    add = np.where(forced, 1e4, np.where(jj <= cur[:, None], 0.0, -1.0))
    c["c_selvalid"] = valid.astype(np.float32).reshape(T // 128, 128, 32).transpose(1, 0, 2).copy()
    c["c_seladd"] = add.astype(np.float32).reshape(T // 128, 128, 32).transpose(1, 0, 2).copy()
    n_cmp = (T - 32) // 16 + 1
    cs = np.arange(n_cmp) * 16
    ss = np.arange(T // 64) * 64
    ov = np.clip(np.minimum(cs[:, None] + 32, ss[None, :] + 64) - np.maximum(cs[:, None], ss[None, :]), 0, None) / 32.0
    ovl = np.zeros((128, 32), np.float32)
    ovl[:n_cmp, :T // 64] = ov
    c["c_ovl"] = ovl
    return c


def host_cols(inp):
    cols = np.zeros((128, NCOLS), np.float32)

    def put(base, v):
        v = np.asarray(v, np.float32).reshape(-1)
        n = v.size // 128
        cols[:, base:base + n] = v.reshape(n, 128).T

    put(CG_MIX, inp["norm_mix_g"]); put(CG_XA, inp["norm_xa_g"]); put(CG_MEM, inp["norm_mem_g"])
    put(CG_FFN, inp["norm_ffn_g"]); put(CG_FIN, inp["final_norm_g"]); put(C_MU, inp["shift_mu"])
    put(C_W0, inp["rw_w0"]); put(C_A0, inp["rw_a0"]); put(C_KK, inp["rw_k_k"]); put(C_KA, inp["rw_k_a"])
    put(C_RK, inp["rw_r_k"]); put(C_LNG, inp["rw_ln_g"]); put(C_LNB, inp["rw_ln_b"])
    return cols


W_NAMES = ["w_in", "rw_w_up", "rw_a_up", "rw_g_up", "nsa_pe_k", "nsa_pe_v", "nsa_ck1", "nsa_ck2", "nsa_cv1", "nsa_cv2",
           "w_up_rw", "w_up_nsa", "w_out", "xa_wq", "xa_wkv", "xa_wo", "ffn_w_gu", "ffn_w_down"]


def make_in_maps(inputs, T, NSEQ, ncores, extra=None):
    consts = host_consts(T)
    cols = host_cols(inputs)
    shared = {k: np.ascontiguousarray(np.asarray(inputs[k], np.float32)[0]) for k in W_NAMES}
    shared.update(consts)
    shared["cols"] = cols
    x = np.asarray(inputs["x"], np.float32)
    mem = np.asarray(inputs["mem"], np.float32)
    maps = []
    for i in range(ncores):
        m = dict(shared)
        m["x"] = np.ascontiguousarray(x[i * NSEQ:(i + 1) * NSEQ])
        m["mem"] = np.ascontiguousarray(mem[i * NSEQ:(i + 1) * NSEQ])
        if extra:
            for k, v in extra.items():
                m[k] = np.ascontiguousarray(v[i * NSEQ:(i + 1) * NSEQ])
        maps.append(m)
    return maps


def kernel(**inputs):
    B, T, _ = inputs["x"].shape
    ncores = 8
    NSEQ = B // ncores
    nc = bass.Bass("TRN2", target_bir_lowering=False)
    bld = Builder(nc, T, NSEQ)
    bld.build()
    maps = make_in_maps(inputs, T, NSEQ, ncores)
    res = run_bass_kernel_spmd(nc, maps, core_ids=list(range(ncores)))
    out = np.concatenate([r["out"] for r in res.results], axis=0)
    return out.astype(np.float32)
```

```python
import contextlib
import math
import numpy as np
import concourse.bass as bass
import concourse.mybir as mybir
from concourse.bass_utils import run_bass_kernel_spmd

F32 = mybir.dt.float32
BF16 = mybir.dt.bfloat16
AF = mybir.ActivationFunctionType
ALU = mybir.AluOpType

ENGS = ["pe", "act", "dve", "pool", "sp"]
SAME_SYNC = {"pe": False, "act": True, "dve": True, "pool": True, "sp": False}
N_DMA_SEMS = 40

D = 1024
NMEM = 256
DFF = 2816
EPS = 1e-6
C_RW, C_Q, C_KC, C_VC, C_KS, C_VS, C_KW, C_VW, C_GN, C_GM = 0, 1792, 2304, 2432, 2560, 2688, 2816, 2944, 3072, 3096
DIN = 5144
CG_MIX, CG_XA, CG_MEM, CG_FFN, CG_FIN, C_MU, C_W0, C_A0, C_KK, C_KA, C_RK, C_LNG, C_LNB, NCOLS = \
    0, 8, 16, 24, 32, 40, 54, 58, 62, 66, 70, 74, 78, 82


class Prog:
    def __init__(self, nc):
        self.nc = nc
        self.stack = contextlib.ExitStack()
        self.ops = {e: [] for e in ENGS}
        self.sems = []
        self.esem = {e: self._newsem("e_" + e) for e in ENGS}
        self.cnt = {e: 0 for e in ENGS}
        self.seen = {e: {} for e in ENGS}
        self.units = {}
        self.dma_sems = [self._newsem("d%d" % i) for i in range(N_DMA_SEMS)]
        self.dma_val = {k: 0 for k in self.dma_sems}
        self.dma_rr = 0
        self.out_events = []
        self.uid = 0
        self.defer = None

    def _newsem(self, name):
        h = self.stack.enter_context(self.nc.semaphore(name))
        self.sems.append(h)
        return len(self.sems) - 1

    def sbuf(self, shape, dtype, name=None):
        self.uid += 1
        return self.stack.enter_context(self.nc.sbuf_tensor(name or ("sb%d" % self.uid), list(shape), dtype))

    def psum(self, shape, dtype, name=None):
        self.uid += 1
        return self.stack.enter_context(self.nc.psum_tensor(name or ("ps%d" % self.uid), list(shape), dtype))

    def _unit(self, u):
        st = self.units.get(u)
        if st is None:
            st = {"w": {}, "r": {}}
            self.units[u] = st
        return st

    def capture(self, thunk):
        assert self.defer is None
        self.defer = []
        thunk()
        ops, self.defer = self.defer, None
        return ops

    def replay(self, ops):
        for a in ops:
            self.op(*a)

    def op(self, eng, fn, reads=(), writes=(), dma=False, is_out=False, par=False):
        if self.defer is not None:
            self.defer.append((eng, fn, tuple(reads), tuple(writes), dma, is_out, par))
            return None
        waits = {}
        own = self.esem[eng]

        def need(ev):
            if ev is None:
                return
            sk, v = ev
            if sk == own and not dma and not SAME_SYNC[eng]:
                return
            if self.seen[eng].get(sk, 0) >= v:
                return
            if waits.get(sk, 0) < v:
                waits[sk] = v

        for u in reads:
            for sk, v in self._unit(u)["w"].items():
                need((sk, v))
        for u in writes:
            st = self._unit(u)
            if not par:
                for sk, v in st["w"].items():
                    need((sk, v))
            for sk, v in st["r"].items():
                need((sk, v))
        if dma:
            k = self.dma_sems[self.dma_rr % len(self.dma_sems)]
            self.dma_rr += 1
            need((k, self.dma_val[k]))
            self.dma_val[k] += 16
            ev = (k, self.dma_val[k])
            inc = (k, 16)
        else:
            self.cnt[eng] += 1
            ev = (own, self.cnt[eng])
            inc = (own, 1)
        for sk, v in waits.items():
            self.seen[eng][sk] = v
        self.ops[eng].append((fn, sorted(waits.items()), inc))
        for u in reads:
            st = self._unit(u)
            if st["r"].get(ev[0], 0) < ev[1]:
                st["r"][ev[0]] = ev[1]
        for u in writes:
            st = self._unit(u)
            if par:
                st["w"][ev[0]] = ev[1]
            else:
                st["w"] = {ev[0]: ev[1]}
            st["r"] = {}
        if is_out:
            self.out_events.append(ev)
        return ev

    def barrier(self, keep=()):
        kept = {u: self.units[u] for u in keep if u in self.units}
        skip = set()
        for st in kept.values():
            for sk, v in st["w"].items():
                if sk in self.dma_val and self.dma_val[sk] == v:
                    skip.add(sk)
        allw = {}
        for e in ENGS:
            if self.cnt[e] > 0:
                allw[self.esem[e]] = self.cnt[e]
        for k in self.dma_sems:
            if self.dma_val[k] > 0 and k not in skip:
                allw[k] = self.dma_val[k]
        for e in ENGS:
            waits = {}
            for sk, v in allw.items():
                if sk == self.esem[e] and not SAME_SYNC[e]:
                    continue
                if self.seen[e].get(sk, 0) < v:
                    waits[sk] = v
                    self.seen[e][sk] = v
            if waits:
                self.ops[e].append((None, sorted(waits.items()), None))
        self.units = kept

    def finish(self):
        waits = {}
        for sk, v in self.out_events:
            if waits.get(sk, 0) < v:
                waits[sk] = v
        self.ops["sp"].append((None, sorted(waits.items()), None))

    def emit(self):
        nc = self.nc
        sems = self.sems
        ops = self.ops

        def run(engobj, lst):
            for fn, waits, inc in lst:
                for sk, v in waits:
                    engobj.wait_ge(sems[sk], v)
                if fn is None:
                    continue
                ins = fn(engobj)
                if inc is not None:
                    ins.then_inc(sems[inc[0]], inc[1])

        with nc.Block() as block:
            @block.tensor
            def _(e):
                run(e, ops["pe"])

            @block.scalar
            def _(e):
                run(e, ops["act"])

            @block.vector
            def _(e):
                run(e, ops["dve"])

            @block.gpsimd
            def _(e):
                run(e, ops["pool"])

            @block.sync
            def _(e):
                run(e, ops["sp"])
        self.stack.close()


class Arena:
    def __init__(self, P, nbytes):
        self.t = P.sbuf([128, nbytes // 4], F32, "arena")
        self.n = nbytes
        self.off = 0
        self.top = nbytes

    def mark(self):
        return self.off

    def release(self, m):
        self.off = m

    def alloc(self, free_shape, dtype, top=False):
        n = 1
        for d in free_shape:
            n *= d
        b = n * (4 if dtype == F32 else 2)
        b = (b + 63) // 64 * 64
        assert self.off + b <= self.top, ("arena overflow", self.off, b, self.top)
        if top:
            self.top -= b
            v = self.t[:, self.top // 4:(self.top + b) // 4]
        else:
            v = self.t[:, self.off // 4:(self.off + b) // 4]
            self.off += b
        if dtype != F32:
            v = v.bitcast(dtype)
        v = v[:, 0:n]
        if len(free_shape) == 2:
            v = v.rearrange("p (a b) -> p a b", a=free_shape[0])
        elif len(free_shape) == 3:
            v = v.rearrange("p (a b c) -> p a b c", a=free_shape[0], b=free_shape[1])
        elif len(free_shape) == 4:
            v = v.rearrange("p (a b c d) -> p a b c d", a=free_shape[0], b=free_shape[1], c=free_shape[2])
        return v


class Builder:
    def __init__(self, nc, T, NSEQ, dbg=None):
        self.nc = nc
        self.T = T
        self.NSEQ = NSEQ
        self.dbg = dbg or {}
        self.P = Prog(nc)
        self.A = Arena(self.P, 204 * 1024)
        self.PS = [self.P.psum([128, 512], F32, "bank%d" % i)[:, :] for i in range(8)]
        self.bank_rr = 0
        self.nrr = 8
        self.bank_set = None
        self.par = False
        self.acc_rr = 0
        self.uid = 0
        self.dq = 0

    def bank(self):
        if self.bank_set is not None:
            bs = self.bank_set
            i = bs[self.bank_rr % len(bs)]
        else:
            i = self.bank_rr % self.nrr
        self.bank_rr += 1
        return i, self.PS[i], "ps%d" % i

    def acc_bank(self):
        i = 6 + (self.acc_rr % 2)
        self.acc_rr += 1
        return i, self.PS[i], "ps%d" % i

    def mm(self, out, lhsT, rhs, start, stop, r, w, **kw):
        self.P.op("pe", lambda e: e.matmul(out, lhsT=lhsT, rhs=rhs, start=start, stop=stop, **kw), reads=r, writes=w)

    def tr(self, out, in_, ident, r, w):
        self.P.op("pe", lambda e: e.transpose(out=out, in_=in_, identity=ident), reads=r, writes=w)

    def act(self, out, in_, func, r, w, bias=None, scale=None, accum=None):
        kw = {}
        if bias is not None:
            kw["bias"] = bias
        if scale is not None:
            kw["scale"] = scale
        if accum is not None:
            kw["accum_out"] = accum
        self.P.op("act", lambda e: e.activation(out=out, in_=in_, func=func, **kw), reads=r, writes=w, par=self.par)

    def tt(self, eng, out, in0, in1, op, r, w):
        self.P.op(eng, lambda e: e.tensor_tensor(out=out, in0=in0, in1=in1, op=op), reads=r, writes=w)

    def ts(self, eng, out, in0, s1, s2, op0, op1, r, w):
        if s2 is None:
            self.P.op(eng, lambda e: e.tensor_scalar(out=out, in0=in0, scalar1=s1, scalar2=None, op0=op0), reads=r, writes=w, par=self.par)
        else:
            self.P.op(eng, lambda e: e.tensor_scalar(out=out, in0=in0, scalar1=s1, scalar2=s2, op0=op0, op1=op1), reads=r, writes=w)

    def stt(self, out, in0, scalar, in1, op0, op1, r, w):
        self.P.op("dve", lambda e: e.scalar_tensor_tensor(out=out, in0=in0, scalar=scalar, in1=in1, op0=op0, op1=op1),
                  reads=r, writes=w)

    def copy(self, eng, out, in_, r, w):
        if eng == "act":
            self.act(out, in_, AF.Copy, r, w)
        else:
            self.P.op(eng, lambda e: e.tensor_copy(out=out, in_=in_), reads=r, writes=w, par=self.par)

    def memset(self, eng, ap, val, w):
        self.P.op(eng, lambda e: e.memset(ap, val), writes=w)

    def recip(self, out, in_, r, w):
        self.P.op("dve", lambda e: e.reciprocal(out=out, in_=in_), reads=r, writes=w)

    def dma(self, eng, out, in_, r, w, is_out=False, par=False, **kw):
        self.P.op(eng, lambda e: e.dma_start(out=out, in_=in_, **kw), reads=r, writes=w, dma=True, is_out=is_out, par=par)

    def wload(self, dst, src2d, kc, w):
        for c in range(kc):
            self.dma("pool", dst[:, c, :], src2d[c * 128:(c + 1) * 128, :], [], w, par=(c > 0))

    def declare(self):
        nc, T, NSEQ = self.nc, self.T, self.NSEQ
        d = {}

        def inp(name, shape):
            d[name] = nc.dram_tensor(name, list(shape), F32, kind="ExternalInput").ap()

        inp("x", [NSEQ, T, D])
        inp("mem", [NSEQ, NMEM, D])
        inp("w_in", [D, DIN])
        inp("rw_w_up", [64, 512]); inp("rw_a_up", [64, 512]); inp("rw_g_up", [128, 512])
        inp("nsa_pe_k", [32, 64]); inp("nsa_pe_v", [32, 64])
        inp("nsa_ck1", [2048, 128]); inp("nsa_ck2", [128, 64]); inp("nsa_cv1", [2048, 128]); inp("nsa_cv2", [128, 64])
        inp("w_up_rw", [512, D]); inp("w_up_nsa", [512, D]); inp("w_out", [D, D])
        inp("xa_wq", [D, D]); inp("xa_wkv", [D, 2 * D]); inp("xa_wo", [D, D])
        inp("ffn_w_gu", [D, 2 * DFF]); inp("ffn_w_down", [DFF, D])
        inp("cols", [128, NCOLS])
        inp("c_identf", [128, 128]); inp("c_bdf", [128, 128])
        inp("c_bf", [128, 7 * 128])
        inp("c_rst", [128, 512])
        inp("c_cos", [128, T]); inp("c_sin", [128, T])
        inp("c_cmpmask", [128, T]); inp("c_eexp", [128, T])
        inp("c_selvalid", [128, T // 128, 32]); inp("c_seladd", [128, T // 128, 32])
        inp("c_ovl", [128, 32])
        if "inject" in self.dbg:
            inp("dbg_rw", [NSEQ, 512, T]); inp("dbg_nsa", [NSEQ, 512, T])
        if "inject_rw" in self.dbg:
            inp("dbg_rw", [NSEQ, 512, T])
        if "dump" in self.dbg:
            for nm in ("dbg_ons", "dbg_orw"):
                d[nm] = nc.dram_tensor(nm, [NSEQ, 512, T], F32, kind="ExternalOutput").ap()
        d["out"] = nc.dram_tensor("out", [NSEQ, T, D], F32, kind="ExternalOutput").ap()
        for name, shape in self.dbg.get("outs", {}).items():
            d[name] = nc.dram_tensor(name, list(shape), F32, kind="ExternalOutput").ap()
        self.d = d

    def setup_consts(self):
        A, d = self.A, self.d
        self.cols = A.alloc([NCOLS], F32)
        self.identf = A.alloc([128], F32)
        self.bdf = A.alloc([128], F32)
        self.cbf = A.alloc([7, 128], BF16)
        self.rst = A.alloc([512], F32)
        self.dma("sp", self.cols, d["cols"][:, :], [], ["cols"])
        self.dma("sp", self.identf, d["c_identf"][:, :], [], ["identf"])
        self.dma("sp", self.bdf, d["c_bdf"][:, :], [], ["bdf"])
        self.dma("sp", self.rst, d["c_rst"][:, :], [], ["rst"])
        self.dma("pool", self.cbf, d["c_bf"].rearrange("p (a b) -> p a b", a=7), [], ["cbf"])
        self.identb = self.cbf[:, 0, :]
        self.bdb = self.cbf[:, 1, :]
        self.onesb = self.cbf[:, 2, :]
        self.m_su = self.cbf[:, 3, :]
        self.m_sl = self.cbf[:, 4, :]
        self.m_iu = self.cbf[:, 5, :]
        self.permb = self.cbf[:, 6, :]
        self.CU = ["cols", "identf", "bdf", "cbf", "rst"]

    def col(self, base, c):
        return self.cols[:, base + c:base + c + 1]

    def tok_norm_T(self, src_rows, ntiles, gbase, dstT, dunit, rawT=None, rawunit=None, keep=(), nobar=False):
        A = self.A
        m = A.mark()
        xst = [A.alloc([D], F32) for _ in range(2)]
        if dstT is not None:
            xs = [A.alloc([D], F32) for _ in range(2)]
            junk = A.alloc([D], BF16)
            sm = A.alloc([8], F32)
        for tt in range(ntiles):
            b = tt % 2
            xu, su = "xst%d" % b, "xs%d" % b
            self.dma("sp", xst[b], src_rows(tt), [], [xu])
            if dstT is not None:
                self.act(junk, xst[b], AF.Square, [xu], ["junk", "sm"], accum=sm[:, 0:1])
                self.act(sm[:, 1:2], sm[:, 0:1], AF.Sqrt, ["sm"], ["sm"], bias=EPS, scale=1.0 / D)
                self.recip(sm[:, 2:3], sm[:, 1:2], ["sm"], ["sm"])
                self.act(xs[b], xst[b], AF.Copy, [xu, "sm"], [su], scale=sm[:, 2:3])
            for half in range(2):
                if dstT is not None:
                    bi, pb, pu = self.bank()
                    for j in range(4):
                        c = half * 4 + j
                        self.tr(pb[:, j * 128:(j + 1) * 128], xs[b][:, c * 128:(c + 1) * 128], self.identf, [su, "identf"], [pu])
                    for j in range(4):
                        c = half * 4 + j
                        if j % 2 == 0:
                            self.act(dstT[:, c, tt * 128:(tt + 1) * 128], pb[:, j * 128:(j + 1) * 128], AF.Copy,
                                     [pu, "cols"], [dunit], scale=self.col(gbase, c))
                        else:
                            self.ts("dve", dstT[:, c, tt * 128:(tt + 1) * 128], pb[:, j * 128:(j + 1) * 128],
                                    self.col(gbase, c), None, ALU.mult, None, [pu, "cols"], [dunit])
                if rawT is not None:
                    bi, pb, pu = self.bank()
                    for j in range(4):
                        c = half * 4 + j
                        self.tr(pb[:, j * 128:(j + 1) * 128], xst[b][:, c * 128:(c + 1) * 128], self.identf, [xu, "identf"], [pu])
                    self.copy("act" if half == 0 else "dve", rawT[:, half * 4:half * 4 + 4, tt * 128:(tt + 1) * 128],
                              pb.rearrange("p (a b) -> p a b", a=4), [pu], [rawunit])
        if not nobar:
            self.P.barrier(keep=keep)
            A.release(m)

    def norm_all(self, gbase, dstT=None, dunit="xnT", fp32_out=None, keep=()):
        A, T = self.A, self.T
        m = A.mark()
        sq = A.alloc([8, 512], BF16)
        rstd = A.alloc([512], F32)
        for blk in range(T // 512):
            sl = slice(blk * 512, (blk + 1) * 512)
            for c in range(8):
                if c % 2 == 0:
                    self.act(sq[:, c, :], self.hT[:, c, sl], AF.Square, ["hT%d" % blk], ["nsq"])
                else:
                    self.tt("pool", sq[:, c, :], self.hT[:, c, sl], self.hT[:, c, sl], ALU.mult, ["hT%d" % blk], ["nsq"])
            bi, pb, pu = self.bank()
            for c in range(8):
                self.mm(pb, self.onesb, sq[:, c, :], c == 0, c == 7, ["nsq", "cbf"], [pu])
            self.act(rstd, pb, AF.Sqrt, [pu], ["nrstd"], bias=EPS, scale=1.0 / D)
            self.recip(rstd, rstd, ["nrstd"], ["nrstd"])
            for c in range(8):
                o = dstT[:, c, sl] if fp32_out is None else fp32_out(blk, c)
                self.stt(o, self.hT[:, c, sl], self.col(gbase, c), rstd, ALU.mult, ALU.mult,
                         ["hT%d" % blk, "cols", "nrstd"], [dunit if fp32_out is None else "fo"])
            if fp32_out is not None:
                yield blk
        self.P.barrier(keep=keep)
        A.release(m)

    def merge1(self, nobar=False):
        A, T, d = self.A, self.T, self.d
        m = A.mark()
        wg1 = A.alloc([8, 512], BF16)
        wg2 = A.alloc([8, 512], BF16)
        wu1 = A.alloc([4, 512], BF16)
        wu2 = A.alloc([4, 512], BF16)
        g1 = A.alloc([512], BF16)
        g2 = A.alloc([512], BF16)
        ta = A.alloc([512], F32)
        tb = A.alloc([512], F32)
        for nh in range(2):
            self.wload(wg1, d["w_in"][:, C_GM + nh * 512:C_GM + nh * 512 + 512], 8, ["wg1"])
            self.wload(wg2, d["w_in"][:, C_GM + 1024 + nh * 512:C_GM + 1024 + nh * 512 + 512], 8, ["wg2"])
            self.wload(wu1, d["w_up_rw"][:, nh * 512:(nh + 1) * 512], 4, ["wu1"])
            self.wload(wu2, d["w_up_nsa"][:, nh * 512:(nh + 1) * 512], 4, ["wu2"])
            for blk in range(T // 512):
                sl = slice(blk * 512, (blk + 1) * 512)
                for nt in range(4):
                    ntile = nh * 4 + nt
                    ns = slice(nt * 128, (nt + 1) * 128)
                    _, p1, u1 = self.bank()
                    for c in range(8):
                        self.mm(p1, wg1[:, c, ns], self.xnT[:, c, sl], c == 0, c == 7, ["wg1", "xnT"], [u1])
                    self.act(g1, p1, AF.Sigmoid, [u1], ["g1"])
                    _, p2, u2 = self.bank()
                    for c in range(4):
                        self.mm(p2, wu1[:, c, ns], self.orwT[:, c, sl], c == 0, c == 3, ["wu1", "orwT"], [u2])
                    self.tt("dve", ta, g1, p2, ALU.mult, ["g1", u2], ["ta"])
                    _, p3, u3 = self.bank()
                    for c in range(8):
                        self.mm(p3, wg2[:, c, ns], self.xnT[:, c, sl], c == 0, c == 7, ["wg2", "xnT"], [u3])
                    self.act(g2, p3, AF.Sigmoid, [u3], ["g2"])
                    _, p4, u4 = self.bank()
                    for c in range(4):
                        self.mm(p4, wu2[:, c, ns], self.onsT[:, c, sl], c == 0, c == 3, ["wu2", "onsT"], [u4])
                    self.tt("dve", tb, g2, p4, ALU.mult, ["g2", u4], ["tb"])
                    self.tt("pool", self.mT[:, ntile, sl], ta, tb, ALU.add, ["ta", "tb"], ["mT"])
        if not nobar:
            self.P.barrier()
            A.release(m)

    def add_proj(self, W, wunit, kc, srcT, sunit, blk, sl):
        for ntile in range(8):
            _, pb, pu = self.bank()
            for c in range(kc):
                self.mm(pb, W[:, c, ntile * 128:(ntile + 1) * 128], srcT[:, c, sl], c == 0, c == kc - 1, [wunit, sunit], [pu])
            hs = self.hT[:, ntile, blk * 512:(blk + 1) * 512]
            self.tt("dve", hs, hs, pb, ALU.add, [pu, "hT%d" % blk], ["hT%d" % blk])

    def merge2(self, s):
        A, T, d = self.A, self.T, self.d
        m = A.mark()
        wo = A.alloc([8, D], BF16)
        self.wload(wo, d["w_out"], 8, ["wo"])
        for blk in range(T // 512):
            self.add_proj(wo, "wo", 8, self.mT, "mT", blk, slice(blk * 512, (blk + 1) * 512))
        self.P.barrier()
        A.release(m)

    def xattn(self, s):
        A, T, d = self.A, self.T, self.d
        m0 = A.mark()
        KT = A.alloc([8, NMEM], BF16)
        V = A.alloc([2, D], BF16)
        wq = A.alloc([8, D], BF16)
        wo = A.alloc([8, D], BF16)
        self.wload(wq, d["xa_wq"], 8, ["wq"])
        self.wload(wo, d["xa_wo"], 8, ["wox"])
        m1 = A.mark()
        memnT = A.alloc([8, NMEM], BF16)
        self.tok_norm_T(lambda tt: d["mem"][s, tt * 128:(tt + 1) * 128, :], NMEM // 128, CG_MEM, memnT, "memnT", keep=["wq", "wox"])
        wkv = A.alloc([8, 2 * D], BF16)
        self.wload(wkv, d["xa_wkv"], 8, ["wkv"])
        for ntile in range(8):
            _, pb, pu = self.bank()
            for c in range(8):
                self.mm(pb[:, 0:NMEM], wkv[:, c, ntile * 128:(ntile + 1) * 128], memnT[:, c, :], c == 0, c == 7, ["wkv", "memnT"], [pu])
            self.copy("act", KT[:, ntile, :], pb[:, 0:NMEM], [pu], ["KT"])
        for mt in range(2):
            for half in range(2):
                _, pb, pu = self.bank()
                for c in range(8):
                    self.mm(pb, memnT[:, c, mt * 128:(mt + 1) * 128], wkv[:, c, D + half * 512:D + (half + 1) * 512],
                            c == 0, c == 7, ["wkv", "memnT"], [pu])
                self.copy("dve", V[:, mt, half * 512:(half + 1) * 512], pb, [pu], ["Vx"])
        self.P.barrier(keep=["wq", "wox"])
        A.release(m1)
        for _ in self.norm_all(CG_XA, self.xnT, keep=["wq", "wox"]):
            pass
        qx = A.alloc([8, 512], BF16)
        PTs = [A.alloc([2, 512], BF16) for _ in range(2)]
        rdens = [A.alloc([512], F32) for _ in range(2)]
        oT = A.alloc([8, 512], BF16)
        for blk in range(T // 512):
            sl = slice(blk * 512, (blk + 1) * 512)
            for ntile in range(8):
                _, pb, pu = self.bank()
                for c in range(8):
                    self.mm(pb, wq[:, c, ntile * 128:(ntile + 1) * 128], self.xnT[:, c, sl], c == 0, c == 7, ["wq", "xnT"], [pu])
                self.copy("act" if ntile % 2 == 0 else "dve", qx[:, ntile, :], pb, [pu], ["qx"])

            def xs(hx):
                PT, ptu_ = PTs[hx % 2], "PTx%d" % (hx % 2)
                for mt in range(2):
                    _, pb, pu = self.bank()
                    for dd in range(2):
                        self.mm(pb, KT[:, hx * 2 + dd, mt * 128:(mt + 1) * 128], qx[:, hx * 2 + dd, :], dd == 0, dd == 1, ["KT", "qx"], [pu])
                    self.act(PT[:, mt, :], pb, AF.Exp, [pu], [ptu_], scale=1.0 / 16.0)

            def xpv(hx):
                PT, ptu_ = PTs[hx % 2], "PTx%d" % (hx % 2)
                rden, ru_ = rdens[hx % 2], "rdenx%d" % (hx % 2)
                _, pb, pu = self.bank()
                for mt in range(2):
                    self.mm(pb, self.onesb, PT[:, mt, :], mt == 0, mt == 1, ["cbf", ptu_], [pu])
                self.recip(rden, pb, [pu], [ru_])
                for dd in range(2):
                    _, pb, pu = self.bank()
                    for mt in range(2):
                        self.mm(pb, V[:, mt, (hx * 2 + dd) * 128:(hx * 2 + dd + 1) * 128], PT[:, mt, :], mt == 0, mt == 1, ["Vx", ptu_], [pu])
                    self.tt("dve", oT[:, hx * 2 + dd, :], pb, rden, ALU.mult, [pu, ru_], ["oTx"])

            xs(0)
            for hx in range(4):
                if hx + 1 < 4:
                    xs(hx + 1)
                xpv(hx)
            self.add_proj(wo, "wox", 8, oT, "oTx", blk, slice(0, 512))
        self.P.barrier()
        A.release(m0)

    def ffn(self):
        A, T, d = self.A, self.T, self.d
        m = A.mark()
        QT = [6, 6, 5, 5]
        NFM = 6
        sets = []
        for i in range(2):
            sets.append((A.alloc([8, NFM * 128], BF16), A.alloc([8, NFM * 128], BF16), A.alloc([NFM, D], BF16)))
        actT = A.alloc([NFM, 512], BF16)
        sg = [A.alloc([512], F32) for _ in range(2)]

        def load(q):
            wg, wu, wd = sets[q % 2]
            f0 = sum(QT[:q]) * 128
            n = QT[q] * 128
            su = "ws%d" % (q % 2)
            self.wload(wg[:, :, 0:n], d["ffn_w_gu"][:, f0:f0 + n], 8, [su])
            self.wload(wu[:, :, 0:n], d["ffn_w_gu"][:, DFF + f0:DFF + f0 + n], 8, [su])
            self.wload(wd[:, 0:QT[q], :], d["ffn_w_down"][f0:f0 + n, :], QT[q], [su])

        load(0)
        load(1)
        for _ in self.norm_all(CG_FFN, self.xnT, keep=["ws0", "ws1"]):
            pass
        for q in range(4):
            wg, wu, wd = sets[q % 2]
            su = "ws%d" % (q % 2)
            NF = QT[q]
            for blk in range(T // 512):
                sl = slice(blk * 512, (blk + 1) * 512)
                for f in range(NF):
                    fs = slice(f * 128, (f + 1) * 128)
                    _, p1, u1 = self.bank()
                    for c in range(8):
                        self.mm(p1, wg[:, c, fs], self.xnT[:, c, sl], c == 0, c == 7, [su, "xnT"], [u1])
                    _, p2, u2 = self.bank()
                    for c in range(8):
                        self.mm(p2, wu[:, c, fs], self.xnT[:, c, sl], c == 0, c == 7, [su, "xnT"], [u2])
                    self.act(sg[f % 2], p1, AF.Silu, [u1], ["sg%d" % (f % 2)])
                    self.tt("dve", actT[:, f, :], sg[f % 2], p2, ALU.mult, ["sg%d" % (f % 2), u2], ["actT"])
                self.add_proj(wd, su, NF, actT, "actT", blk, slice(0, 512))
            if q + 2 < 4:
                load(q + 2)
        self.P.barrier()
        A.release(m)

    def final_out(self, s):
        A, T, d, P = self.A, self.T, self.d, self.P
        m = A.mark()
        yfs = [A.alloc([8, 512], F32) for _ in range(2)]
        ost = [A.alloc([D], F32) for _ in range(2)]
        sq = A.alloc([8, 512], BF16)
        rstd = A.alloc([512], F32)
        NBK = T // 512

        def norm_blk(blk):
            yf, yu = yfs[blk % 2], "fo%d" % (blk % 2)
            sl = slice(blk * 512, (blk + 1) * 512)
            for c in range(8):
                if c % 2 == 0:
                    self.act(sq[:, c, :], self.hT[:, c, sl], AF.Square, ["hT%d" % blk], ["nsq"])
                else:
                    self.tt("pool", sq[:, c, :], self.hT[:, c, sl], self.hT[:, c, sl], ALU.mult, ["hT%d" % blk], ["nsq"])
            _, pb, pu = self.bank()
            for c in range(8):
                self.mm(pb, self.onesb, sq[:, c, :], c == 0, c == 7, ["nsq", "cbf"], [pu])
            self.act(rstd, pb, AF.Sqrt, [pu], ["nrstd"], bias=EPS, scale=1.0 / D)
            self.recip(rstd, rstd, ["nrstd"], ["nrstd"])
            for c in range(8):
                self.stt(yf[:, c, :], self.hT[:, c, sl], self.col(CG_FIN, c), rstd, ALU.mult, ALU.mult,
                         ["hT%d" % blk, "cols", "nrstd"], [yu])

        kk = [0]

        def out_blk(blk):
            yf, yu = yfs[blk % 2], "fo%d" % (blk % 2)
            for t4 in range(4):
                b = kk[0] % 2
                kk[0] += 1
                for half in range(2):
                    _, pb, pu = self.bank()
                    for j in range(4):
                        c = half * 4 + j
                        self.tr(pb[:, j * 128:(j + 1) * 128], yf[:, c, t4 * 128:(t4 + 1) * 128], self.identf, [yu, "identf"], [pu])
                    self.copy("act" if half == 0 else "dve", ost[b][:, half * 512:(half + 1) * 512], pb, [pu], ["ost%d" % b])
                t0 = blk * 512 + t4 * 128
                self.dma("sp", d["out"][s, t0:t0 + 128, :], ost[b], ["ost%d" % b], [], is_out=True)

        def mergeo(a_, b_):
            out = []
            ia = ib = 0
            na, nb = len(a_), len(b_)
            while ia < na or ib < nb:
                fa = ia / na if ia < na else 2.0
                fb = ib / nb if ib < nb else 2.0
                if fa <= fb:
                    out.append(a_[ia]); ia += 1
                else:
                    out.append(b_[ib]); ib += 1
            return out

        self.bank_set = [6, 7]
        norm_blk(0)
        for blk in range(NBK):
            self.bank_set = [0, 1, 2, 3, 4, 5]
            oa = P.capture(lambda: out_blk(blk))
            ob = []
            if blk + 1 < NBK:
                self.bank_set = [6, 7]
                ob = P.capture(lambda: norm_blk(blk + 1))
            P.replay(mergeo(oa, ob))
        self.bank_set = None
        A.release(m)

    def build(self):
        A, T, d = self.A, self.T, None
        self.declare()
        d = self.d
        self.setup_consts()
        base = A.mark()
        for s in range(self.NSEQ):
            A.release(base)
            A.top = A.n
            self.xnTz = A.alloc([8, T + 32], BF16)
            self.xnT = self.xnTz[:, :, 32:T + 32]
            self.memset("pool", self.xnTz[:, :, 0:32], 0.0, ["xnTz0"])
            self.orwT = A.alloc([4, T], BF16, top=True)
            self.tok_norm_T(lambda tt: d["x"][s, tt * 128:(tt + 1) * 128, :], T // 128, CG_MIX, self.xnT, "xnT")
            if "inject" in self.dbg:
                self.onsT = A.alloc([4, T], BF16, top=True)
                self.dma("pool", self.orwT, d["dbg_rw"][s].rearrange("(c p) t -> p c t", p=128), [], ["orwT"])
                self.dma("pool", self.onsT, d["dbg_nsa"][s].rearrange("(c p) t -> p c t", p=128), [], ["onsT"])
            else:
                if "inject_rw" in self.dbg:
                    self.dma("pool", self.orwT, d["dbg_rw"][s].rearrange("(c p) t -> p c t", p=128), [], ["orwT"])
                else:
                    self.rwkv(s)
                self.onsT = A.alloc([4, T], BF16, top=True)
                self.nsa(s)
            if "dump" in self.dbg:
                self.dma("pool", d["dbg_ons"][s].rearrange("(c p) t -> p c t", p=128), self.onsT, ["onsT"], [], is_out=True)
                self.dma("pool", d["dbg_orw"][s].rearrange("(c p) t -> p c t", p=128), self.orwT, ["orwT"], [], is_out=True)
            self.mT = A.alloc([8, T], BF16, top=True)
            self.hT = A.alloc([8, T], F32)
            mk_ = A.mark()
            if self.dbg.get("ovl_reload", True):
                self.bank_set = [0, 1, 2, 3, 4, 5]
                o_m = self.P.capture(lambda: self.merge1(nobar=True))
                self.bank_set = [6, 7]
                o_r = self.P.capture(lambda: self.tok_norm_T(lambda tt: d["x"][s, tt * 128:(tt + 1) * 128, :], T // 128, 0, None, None,
                                                             rawT=self.hT, rawunit="hTall", nobar=True))
                self.bank_set = None
                mg = []
                ia = ib = 0
                while ia < len(o_m) or ib < len(o_r):
                    fa = ia / len(o_m) if ia < len(o_m) else 2.0
                    fb = ib / len(o_r) if ib < len(o_r) else 2.0
                    if fa <= fb:
                        mg.append(o_m[ia]); ia += 1
                    else:
                        mg.append(o_r[ib]); ib += 1
                self.P.replay(mg)
                self.P.barrier()
                A.release(mk_)
            else:
                self.merge1()
                self.tok_norm_T(lambda tt: d["x"][s, tt * 128:(tt + 1) * 128, :], T // 128, 0, None, None,
                                rawT=self.hT, rawunit="hTall")
            self.merge2(s)
            A.top = A.n
            self.xattn(s)
            self.ffn()
            self.final_out(s)
            self.P.barrier()
        self.P.finish()
        self.P.emit()


    def rwkv(self, s):
        A, T, d, P = self.A, self.T, self.d, self.P
        TB = 256
        NCK = TB // 128
        C0 = math.exp(-0.5)
        self.nrr = 6
        m0 = A.mark()
        wrw = A.alloc([8, 1792], BF16)
        self.wload(wrw, d["w_in"][:, 0:1792], 8, ["wrw"])
        lw = A.alloc([512], BF16)
        wlg = A.alloc([512], BF16)
        lwA = A.alloc([512], BF16)
        self.memset("pool", lw, 0.0, ["lw"])
        self.memset("pool", lwA, 0.0, ["lwA"])
        self.dma("pool", lw[0:64, :], d["rw_w_up"][:, :], [], ["lw"])
        self.dma("pool", lwA[64:128, :], d["rw_a_up"][:, :], [], ["lwA"])
        self.dma("pool", wlg, d["rw_g_up"][:, :], [], ["wlg"])
        Hf = A.alloc([4, 128], F32)
        Hb = A.alloc([4, 128], BF16)
        self.memset("pool", Hf, 0.0, ["Hf"])
        self.memset("pool", Hb, 0.0, ["Hb"])
        xwxa = A.alloc([TB], F32)
        xg = A.alloc([TB], F32)
        lx = A.alloc([TB], BF16)
        sxg = A.alloc([TB], BF16)
        rT = A.alloc([TB], F32)
        kT = A.alloc([TB], F32)
        vT = A.alloc([TB], F32)
        f = {nm: A.alloc([TB], F32) for nm in ("sg", "eta", "cs", "csx", "igam", "gamx", "kkr", "rn", "kk", "t1", "kp", "bt")}
        sqb = A.alloc([TB], BF16)
        rkb = A.alloc([TB], BF16)
        vb = A.alloc([4, TB], BF16)
        SH = []
        for _par in range(2):
            SH.append(dict(gam=A.alloc([4, TB], F32), AT=A.alloc([4, TB], BF16), RT=A.alloc([4, TB], BF16),
                           BT=A.alloc([4, TB], BF16), KTt=A.alloc([4, TB], BF16), gT=A.alloc([4, TB], BF16),
                           bonT=A.alloc([4, TB], BF16), tokV=A.alloc([NCK, 4, 128], BF16),
                           tokB=A.alloc([NCK, 4, 128], BF16), tokK=A.alloc([NCK, 4, 128], BF16)))
        Pm = [[[A.alloc([4, 128], BF16) for _ in range(2)] for _ in range(2)] for _ in range(NCK)]
        PTm = [[[A.alloc([4, 128], BF16) for _ in range(2)] for _ in range(2)] for _ in range(NCK)]
        Xm = [[[A.alloc([4, 128], BF16) for _ in range(2)] for _ in range(2)] for _ in range(NCK)]
        Lak = [[A.alloc([4, 128], BF16) for _ in range(2)] for _ in range(NCK)]
        Mrb = [[A.alloc([4, 128], BF16) for _ in range(2)] for _ in range(NCK)]
        Mrk = [[A.alloc([4, 128], BF16) for _ in range(2)] for _ in range(NCK)]
        W1s = A.alloc([512], BF16)
        Us = A.alloc([512], BF16)
        Ht = A.alloc([4, 128], F32)
        ysq = A.alloc([512], F32)
        yc = A.alloc([512], F32)
        ynb = A.alloc([512], BF16)
        st = A.alloc([64], F32)
        zt = A.alloc([128], F32)

        def bc4(ap128):
            return ap128.unsqueeze(1).to_broadcast([128, 4, 128])

        def v4(ap512):
            return ap512.rearrange("p (h c) -> p h c", h=4)

        carry = A.alloc([14], F32)
        self.memset("pool", carry, 0.0, ["carry"])
        praw = [A.alloc([TB + 1], F32) for _ in range(2)]
        dtmp = A.alloc([TB], F32)
        pr = [0]

        def proj_shift(ct, tok0, dst, dunit):
            b = pr[0] % 2
            pr[0] += 1
            pw, pwu = praw[b], "praw%d" % b
            _, pb, pu = self.bank()
            for c in range(8):
                self.mm(pb[:, 0:TB], wrw[:, c, ct * 128:(ct + 1) * 128], self.xnT[:, c, tok0:tok0 + TB], c == 0, c == 7, ["wrw", "xnT"], [pu])
            self.copy("pool", pw[:, 0:1], carry[:, ct:ct + 1], ["carry"], [pwu])
            self.copy("act", pw[:, 1:TB + 1], pb[:, 0:TB], [pu], [pwu])
            self.copy("pool", carry[:, ct:ct + 1], pw[:, TB:TB + 1], [pwu], ["carry"])
            self.tt("pool", dtmp, pw[:, 0:TB], pw[:, 1:TB + 1], ALU.subtract, [pwu], ["dtmp"])
            self.stt(dst, dtmp, self.col(C_MU, ct), pw[:, 1:TB + 1], ALU.mult, ALU.add, ["dtmp", pwu, "cols"], [dunit])

        SHK = ("gam", "AT", "RT", "BT", "KTt", "gT", "bonT", "tokV", "tokB", "tokK")

        def front(blk):
            par = blk % 2
            tok0 = blk * TB
            gam, AT, RT, BT, KTt, gT, bonT, tokV, tokB, tokK = (SH[par][k_] for k_ in SHK)
            ugam, uAT, uRT, uBT, uKTt, ugT, ubonT, utokV, utokB, utokK = ("%s%d" % (k_, par) for k_ in SHK)
            proj_shift(12, tok0, xwxa, "xwxa")
            proj_shift(13, tok0, xg, "xg")
            self.act(lx[0:64, :], xwxa[0:64, :], AF.Tanh, ["xwxa"], ["lx"])
            self.copy("pool", lx[64:128, :], xwxa[64:128, :], ["xwxa"], ["lx"])
            self.act(sxg, xg, AF.Sigmoid, ["xg"], ["sxg"])
            for hp in range(4):
                hs = slice(hp * 128, (hp + 1) * 128)
                proj_shift(hp, tok0, rT, "rT")
                proj_shift(4 + hp, tok0, kT, "kT")
                proj_shift(8 + hp, tok0, vT, "vT")
                _, pb, pu = self.bank()
                self.mm(pb[:, 0:TB], lw[:, hs], lx, True, True, ["lw", "lx"], [pu])
                self.act(f["sg"], pb[:, 0:TB], AF.Sigmoid, [pu, "cols"], ["sg"], bias=self.col(C_W0, hp))
                _, pb, pu = self.bank()
                self.mm(pb[:, 0:TB], lwA[:, hs], lx, True, True, ["lwA", "lx"], [pu])
                self.act(f["eta"], pb[:, 0:TB], AF.Sigmoid, [pu, "cols"], ["eta"], bias=self.col(C_A0, hp))
                _, pb, pu = self.bank()
                self.mm(pb[:, 0:TB], wlg[:, hs], sxg, True, True, ["wlg", "sxg"], [pu])
                self.copy("act", gT[:, hp, :], pb[:, 0:TB], [pu], [ugT])
                self.P.op("dve", lambda e: e.tensor_tensor_scan(out=f["cs"], data0=self.rst[:, 0:TB], data1=f["sg"], initial=0.0,
                                                                op0=ALU.mult, op1=ALU.add), reads=["sg", "rst"], writes=["cs"])
                self.act(gam[:, hp, :], f["cs"], AF.Exp, ["cs"], [ugam], scale=-C0)
                self.act(f["igam"], f["cs"], AF.Exp, ["cs"], ["igam"], scale=C0)
                self.tt("pool", f["csx"], f["cs"], f["sg"], ALU.subtract, ["cs", "sg"], ["csx"])
                self.act(f["gamx"], f["csx"], AF.Exp, ["csx"], ["gamx"], scale=-C0)
                self.ts("pool", f["kkr"], kT, self.col(C_KK, hp), None, ALU.mult, None, ["kT", "cols"], ["kkr"])
                self.act(sqb, f["kkr"], AF.Square, ["kkr"], ["sqb"])
                _, pb, pu = self.bank()
                self.mm(pb[:, 0:TB], self.bdb, sqb, True, True, ["cbf", "sqb"], [pu])
                self.ts("dve", f["rn"], pb[:, 0:TB], 1e-12, None, ALU.max, None, [pu], ["rn"])
                self.act(f["rn"], f["rn"], AF.Sqrt, ["rn"], ["rn"])
                self.recip(f["rn"], f["rn"], ["rn"], ["rn"])
                self.tt("pool", f["kk"], f["kkr"], f["rn"], ALU.mult, ["kkr", "rn"], ["kk"])
                self.ts("dve", f["t1"], f["eta"], -1.0, self.col(C_KA, hp), ALU.add, ALU.mult, ["eta", "cols"], ["t1"])
                self.stt(f["kp"], f["t1"], 1.0, kT, ALU.add, ALU.mult, ["t1", "kT"], ["kp"])
                self.stt(AT[:, hp, :], f["kk"], -1.0, f["gamx"], ALU.mult, ALU.mult, ["kk", "gamx"], [uAT])
                self.tt("pool", RT[:, hp, :], rT, gam[:, hp, :], ALU.mult, ["rT", ugam], [uRT])
                self.tt("pool", f["bt"], f["kk"], f["eta"], ALU.mult, ["kk", "eta"], ["bt"])
                self.tt("pool", BT[:, hp, :], f["bt"], f["igam"], ALU.mult, ["bt", "igam"], [uBT])
                self.tt("pool", KTt[:, hp, :], f["kp"], f["igam"], ALU.mult, ["kp", "igam"], [uKTt])
                self.copy("act", vb[:, hp, :], vT, ["vT"], ["vb"])
                self.stt(rkb, rT, self.col(C_RK, hp), f["kp"], ALU.mult, ALU.mult, ["rT", "kp", "cols"], ["rkb"])
                _, pb, pu = self.bank()
                self.mm(pb[:, 0:TB], self.bdb, rkb, True, True, ["cbf", "rkb"], [pu])
                self.tt("dve", bonT[:, hp, :], pb[:, 0:TB], vT, ALU.mult, [pu, "vT"], [ubonT])
            for cc in range(NCK):
                cs_ = slice(cc * 128, (cc + 1) * 128)
                for src, su, dst, du in ((vb, "vb", tokV, utokV), (BT, uBT, tokB, utokB), (KTt, uKTt, tokK, utokK)):
                    _, pb, pu = self.bank()
                    pbb = pb.bitcast(BF16)
                    for hp in range(4):
                        self.tr(pbb[:, hp * 128:(hp + 1) * 128], src[:, hp, cs_], self.identb, [su, "cbf"], [pu])
                    self.copy("act" if du != utokB else "dve", dst[:, cc, :, :], pbb[:, 0:512].rearrange("p (a b) -> p a b", a=4), [pu], [du])
        def back(blk, ccs, do_inv, do_chain):
            par = blk % 2
            tok0 = blk * TB
            gam, AT, RT, BT, KTt, gT, bonT, tokV, tokB, tokK = (SH[par][k_] for k_ in SHK)
            ugam, uAT, uRT, uBT, uKTt, ugT, ubonT, utokV, utokB, utokK = ("%s%d" % (k_, par) for k_ in SHK)
            chains = [(cc, e) for cc in ccs for e in range(2)] if do_inv else []
            for cc, e in chains:
                cs_ = slice(cc * 128, (cc + 1) * 128)
                es = slice(e * 64, (e + 1) * 64)
                cu_ = "c%de%d" % (cc, e)

                def score(lh, lu, rh, ru):
                    _, pb, pu = self.bank()
                    for hp in range(4):
                        self.mm(pb[:, hp * 128:(hp + 1) * 128], lh[es, hp, cs_], rh[es, hp, cs_], True, True, [lu, ru], [pu])
                    return pb, pu

                P0, PT0, X0 = Pm[cc][e][0], PTm[cc][e][0], Xm[cc][e][0]
                pb, pu = score(BT, uBT, AT, uAT)
                self.copy("act", P0, v4(pb), [pu], ["P0" + cu_])
                self.tt("pool", P0, P0, bc4(self.m_su), ALU.mult, ["P0" + cu_, "cbf"], ["P0" + cu_])
                pb, pu = score(AT, uAT, BT, uBT)
                self.copy("act", PT0, v4(pb), [pu], ["PT0" + cu_])
                self.tt("pool", PT0, PT0, bc4(self.m_sl), ALU.mult, ["PT0" + cu_, "cbf"], ["PT0" + cu_])
                pb, pu = score(KTt, uKTt, AT, uAT)
                self.copy("act", Lak[cc][e], v4(pb), [pu], ["Lak" + cu_])
                self.tt("pool", Lak[cc][e], Lak[cc][e], bc4(self.m_su), ALU.mult, ["Lak" + cu_, "cbf"], ["Lak" + cu_])
                pb, pu = score(BT, uBT, RT, uRT)
                self.tt("dve", Mrb[cc][e], v4(pb), bc4(self.m_iu), ALU.mult, [pu, "cbf"], ["Mrb" + cu_])
                pb, pu = score(KTt, uKTt, RT, uRT)
                self.tt("dve", Mrk[cc][e], v4(pb), bc4(self.m_iu), ALU.mult, [pu, "cbf"], ["Mrk" + cu_])
                self.tt("pool", X0, P0, bc4(self.identb), ALU.add, ["P0" + cu_, "cbf"], ["X0" + cu_])
            cur = 0
            for lvl in (range(1, 7) if do_inv else []):
                nxt = 1 - cur
                for cc, e in chains:
                    cu_ = "c%de%d" % (cc, e)
                    Pc_, PTc_, Xc_ = Pm[cc][e][cur], PTm[cc][e][cur], Xm[cc][e][cur]
                    Pn_, PTn_, Xn_ = Pm[cc][e][nxt], PTm[cc][e][nxt], Xm[cc][e][nxt]
                    cu, nu = "%d%s" % (cur, cu_), "%d%s" % (nxt, cu_)
                    _, pb, pu = self.bank()
                    for hp in range(4):
                        self.mm(pb[:, hp * 128:(hp + 1) * 128], Pc_[:, hp, :], PTc_[:, hp, :], True, True, ["P" + cu, "PT" + cu], [pu])
                    self.copy("act", PTn_, v4(pb), [pu], ["PT" + nu])
                    if lvl < 6:
                        _, pb, pu = self.bank()
                        for hp in range(4):
                            self.mm(pb[:, hp * 128:(hp + 1) * 128], PTc_[:, hp, :], Pc_[:, hp, :], True, True, ["P" + cu, "PT" + cu], [pu])
                        self.copy("act" if (cc + e) % 2 else "dve", Pn_, v4(pb), [pu], ["P" + nu])
                for cc, e in chains:
                    cu_ = "c%de%d" % (cc, e)
                    Xc_, Xn_, PTn_ = Xm[cc][e][cur], Xm[cc][e][nxt], PTm[cc][e][nxt]
                    cu, nu = "%d%s" % (cur, cu_), "%d%s" % (nxt, cu_)
                    _, pb, pu = self.bank()
                    for hp in range(4):
                        self.mm(pb[:, hp * 128:(hp + 1) * 128], PTn_[:, hp, :], Xc_[:, hp, :], True, True, ["PT" + nu, "X" + cu], [pu])
                    self.tt("dve", Xn_, v4(pb), Xc_, ALU.add, [pu, "X" + cu], ["X" + nu])
                cur = nxt
            assert cur == 0
            for cc in (ccs if do_chain else []):
                cs_ = slice(cc * 128, (cc + 1) * 128)
                gtok = tok0 + cc * 128
                _, pw1, w1u = self.acc_bank()
                for hp in range(4):
                    self.mm(pw1[:, hp * 128:(hp + 1) * 128], AT[:, hp, cs_], Hb[:, hp, :], hp == 0, False, [uAT, "Hb"], [w1u], skip_group_check=True)
                _, py, yu = self.acc_bank()
                for hp in range(4):
                    self.mm(py[:, hp * 128:(hp + 1) * 128], RT[:, hp, cs_], Hb[:, hp, :], hp == 0, False, [uRT, "Hb"], [yu], skip_group_check=True)
                for hp in range(4):
                    for e in range(2):
                        o = hp * 128 + e * 64
                        self.mm(pw1[:, o:o + 64], Lak[cc][e][:, hp, :], tokV[:, cc, hp, e * 64:(e + 1) * 64], False, hp == 3 and e == 1,
                                ["Lakc%de%d" % (cc, e), utokV], [w1u], skip_group_check=True)
                self.copy("act", W1s, pw1, [w1u], ["W1s"])
                _, pu_, uu = self.bank()
                for hp in range(4):
                    for e in range(2):
                        o = hp * 128 + e * 64
                        self.mm(pu_[:, o:o + 64], Xm[cc][e][0][:, hp, :], W1s[:, o:o + 64], True, True, ["X0c%de%d" % (cc, e), "W1s"], [uu])
                self.copy("act", Us, pu_, [uu], ["Us"])
                for hp in range(4):
                    for e in range(2):
                        o = hp * 128 + e * 64
                        self.mm(py[:, o:o + 64], Mrb[cc][e][:, hp, :], Us[:, o:o + 64], False, False, ["Mrbc%de%d" % (cc, e), "Us"], [yu], skip_group_check=True)
                        self.mm(py[:, o:o + 64], Mrk[cc][e][:, hp, :], tokV[:, cc, hp, e * 64:(e + 1) * 64], False, hp == 3 and e == 1,
                                ["Mrkc%de%d" % (cc, e), utokV], [yu], skip_group_check=True)
                _, pd, du_ = self.bank()
                for hp in range(4):
                    hs = slice(hp * 128, (hp + 1) * 128)
                    self.mm(pd[:, hs], tokB[:, cc, hp, :], Us[:, hs], True, False, [utokB, "Us"], [du_])
                    self.mm(pd[:, hs], tokK[:, cc, hp, :], tokV[:, cc, hp, :], False, True, [utokK, utokV], [du_])
                self.tt("dve", Ht, v4(pd), Hf, ALU.add, [du_, "Hf"], ["Ht"])
                for hp in range(4):
                    self.stt(Hf[:, hp, :], Ht[:, hp, :], gam[:, hp, cc * 128 + 127:cc * 128 + 128], self.bdf, ALU.mult, ALU.mult,
                             ["Ht", ugam, "bdf"], ["Hf"])
                self.copy("act", Hb, Hf, ["Hf"], ["Hb"])
                y3 = py.rearrange("p (h c) -> p h c", h=8)
                s1, s2, mean, msq, var = (st[:, i * 8:(i + 1) * 8] for i in range(5))
                self.P.op("dve", lambda e, s1=s1, y3=y3: e.tensor_reduce(out=s1, in_=y3, axis=mybir.AxisListType.X, op=ALU.add),
                          reads=[yu], writes=["st"])
                self.act(ysq, py, AF.Square, [yu], ["ysq"])
                self.P.op("dve", lambda e, s2=s2: e.tensor_reduce(out=s2, in_=ysq.rearrange("p (h c) -> p h c", h=8),
                                                                  axis=mybir.AxisListType.X, op=ALU.add), reads=["ysq"], writes=["st"])
                self.ts("dve", mean, s1, 1.0 / 64, None, ALU.mult, None, ["st"], ["st"])
                self.tt("dve", msq, mean, mean, ALU.mult, ["st"], ["st"])
                self.stt(var, s2, 1.0 / 64, msq, ALU.mult, ALU.subtract, ["st"], ["st"])
                self.act(var, var, AF.Sqrt, ["st"], ["st"], bias=64e-5)
                self.recip(var, var, ["st"], ["st"])
                yc3 = yc.rearrange("p (h c) -> p h c", h=8)
                self.tt("dve", yc3, y3, mean.unsqueeze(2).to_broadcast([128, 8, 64]), ALU.subtract, [yu, "st"], ["yc"])
                self.tt("pool", ynb.rearrange("p (h c) -> p h c", h=8), yc3, var.unsqueeze(2).to_broadcast([128, 8, 64]), ALU.mult,
                        ["yc", "st"], ["ynb"])
                _, pt, ptu = self.bank()
                ptb = pt.bitcast(BF16)
                for hp in range(4):
                    self.tr(ptb[:, hp * 128:(hp + 1) * 128], ynb[:, hp * 128:(hp + 1) * 128], self.identb, ["ynb", "cbf"], [ptu])
                for hp in range(4):
                    self.act(zt, ptb[:, hp * 128:(hp + 1) * 128], AF.Identity, [ptu, "cols"], ["zt"],
                             bias=self.col(C_LNB, hp), scale=self.col(C_LNG, hp))
                    self.tt("pool", zt, zt, bonT[:, hp, cs_], ALU.add, ["zt", ubonT], ["zt"])
                    self.tt("pool", self.orwT[:, hp, gtok:gtok + 128], zt, gT[:, hp, cs_], ALU.mult, ["zt", ugT], ["orwT"])
        def merge(a_, b_):
            out = []
            ia = ib = 0
            na, nb = len(a_), len(b_)
            while ia < na or ib < nb:
                fa = ia / na if ia < na else 2.0
                fb = ib / nb if ib < nb else 2.0
                if fa <= fb:
                    out.append(a_[ia]); ia += 1
                else:
                    out.append(b_[ib]); ib += 1
            return out

        NB = T // TB

        def cap(bs, fn):
            self.bank_set = bs
            return P.capture(fn)

        self.bank_set = [0, 1]
        front(0)
        self.bank_set = [2, 3, 4]
        back(0, [0], True, False)
        NC_ = NB * NCK
        for c in range(NC_):
            blk, cc = divmod(c, NCK)
            o_chain = cap([5], lambda: back(blk, [cc], False, True))
            o_bg = []
            if cc == 0 and blk + 1 < NB:
                o_bg = cap([0, 1], lambda: front(blk + 1))
            if c + 1 < NC_:
                blk2, cc2 = divmod(c + 1, NCK)
                o_inv = cap([2, 3, 4], lambda: back(blk2, [cc2], True, False))
                o_bg = merge(o_inv, o_bg)
            P.replay(merge(o_chain, o_bg))
        self.bank_set = None
        P.barrier()
        A.release(m0)
        self.nrr = 8


    def nsa(self, s):
        A, T, d, P = self.A, self.T, self.d, self.P
        NT = T // 128
        NCMP = (T - 32) // 16 + 1
        SC = 0.125
        self.nrr = 6
        win3 = d["w_in"].rearrange("(c p) n -> p c n", p=128)
        m0 = A.mark()
        qT = A.alloc([4, T], BF16)
        qrT = A.alloc([4, T], BF16)
        ks2 = A.alloc([2, 2, T], BF16)
        kw2 = A.alloc([2, 2, T], BF16)
        Vs = A.alloc([NT, 2, 65], BF16)
        Vw = A.alloc([NT, 2, 65], BF16)
        gates = A.alloc([NT, 24], F32)
        kc2 = A.alloc([2, 2, 128], BF16)
        Vc = A.alloc([2, 97], BF16)
        ovl = A.alloc([32], F32)
        self.dma("sp", ovl, d["c_ovl"][:, :], [], ["ovl"])
        self.memset("pool", ks2, 0.0, ["ks2"])
        self.memset("pool", kw2, 0.0, ["kw2"])
        self.memset("pool", kc2, 0.0, ["kc2"])
        self.memset("pool", Vs[:, :, :, 64:65], 1.0, ["Vs"])
        self.memset("pool", Vw[:, :, :, 64:65], 1.0, ["Vw"])
        self.memset("pool", Vc[:, :, 64:65], 1.0, ["Vc"])
        for g in range(2):
            self.copy("pool", Vc[:, g, 65:97], ovl, ["ovl"], ["Vc"])
        m1 = A.mark()
        kcT = A.alloc([T], BF16)
        vcT = A.alloc([T], BF16)
        m1b = A.mark()
        NW = 1304 + 512
        wn = A.alloc([8, NW], BF16)
        self.wload(wn[:, :, 0:1304], d["w_in"][:, C_Q:C_Q + 1304], 8, ["wn"])
        W_KC, W_VC, W_KS, W_VS, W_KW, W_VW, W_GN, W_KSD, W_KWD = 512, 640, 768, 896, 1024, 1152, 1280, 1304, 1560
        i_ = 0
        for g in range(2):
            for rep in range(2):
                for (dbase, sbase) in ((W_KSD, W_KS), (W_KWD, W_KW)):
                    self.copy(("act", "dve", "pool")[i_ % 3], wn[:, :, dbase + g * 128 + rep * 64:dbase + g * 128 + rep * 64 + 64],
                              wn[:, :, sbase + g * 64:sbase + g * 64 + 64], ["wn"], ["wnd"])
                    i_ += 1
        m2 = A.mark()
        cosT = A.alloc([512], F32)
        sinT = A.alloc([512], F32)
        kraw = A.alloc([512], BF16)
        t1 = A.alloc([512], F32)
        t2 = A.alloc([512], F32)

        def proj(col0, blk, out, ounit, eng="act"):
            sl = slice(blk * 512, (blk + 1) * 512)
            _, pb, pu = self.bank()
            for c in range(8):
                self.mm(pb, wn[:, c, col0:col0 + 128], self.xnT[:, c, sl], c == 0, c == 7, ["wn", "wnd", "xnT"], [pu])
            self.copy(eng, out, pb, [pu], [ounit])

        def rope(src, sunit, dst, dunit, blk, split=None):
            sl = slice(blk * 512, (blk + 1) * 512)
            _, pb, pu = self.bank()
            self.mm(pb, self.permb, src, True, True, ["cbf", sunit], [pu])
            self.tt("dve", t1, src, cosT, ALU.mult, [sunit, "rope"], ["t1"])
            self.tt("dve", t2, pb, sinT, ALU.mult, [pu, "rope"], ["t2"])
            if split is None:
                self.tt("pool", dst, t1, t2, ALU.add, ["t1", "t2"], [dunit])
            else:
                for e in range(2):
                    es = slice(e * 64, (e + 1) * 64)
                    self.tt("pool", split[es, e, sl], t1[es, :], t2[es, :], ALU.add, ["t1", "t2"], [dunit])

        for blk in range(T // 512):
            sl = slice(blk * 512, (blk + 1) * 512)
            self.dma("sp", cosT, d["c_cos"][:, sl], [], ["rope"])
            self.dma("sp", sinT, d["c_sin"][:, sl], [], ["rope"])
            for hp in range(4):
                proj(hp * 128, blk, qT[:, hp, sl], "qT")
                rope(qT[:, hp, sl], "qT", qrT[:, hp, sl], "qrT", blk)
            for g in range(2):
                proj(W_KSD + g * 128, blk, kraw, "kraw")
                rope(kraw, "kraw", None, "ks2", blk, split=ks2[:, g])
                proj(W_KWD + g * 128, blk, kraw, "kraw")
                rope(kraw, "kraw", None, "kw2", blk, split=kw2[:, g])
        P.barrier()
        A.release(m2)
        if self.dbg.get("nsa_stop", 99) <= 1:
            A.release(m0); self.nrr = 8; return
        vT = A.alloc([512], BF16)
        for (col0, Vd, vu) in ((W_VS, Vs, "Vs"), (W_VW, Vw, "Vw")):
            for blk in range(T // 512):
                proj(col0, blk, vT, "vT")
                _, pb, pu = self.bank()
                pbb = pb.bitcast(BF16)
                for j in range(4):
                    self.tr(pbb[:, j * 128:(j + 1) * 128], vT[:, j * 128:(j + 1) * 128], self.identb, ["vT", "cbf"], [pu])
                for j in range(4):
                    tt_ = blk * 4 + j
                    self.copy("act" if j % 2 == 0 else "dve", Vd[:, tt_, :, 0:64],
                              pbb[:, j * 128:(j + 1) * 128].rearrange("p (g c) -> p g c", g=2), [pu], [vu])
        for tt_ in range(NT):
            _, pb, pu = self.bank()
            for c in range(8):
                self.mm(pb[:, 0:24], self.xnT[:, c, tt_ * 128:(tt_ + 1) * 128], wn[:, c, W_GN:W_GN + 24], c == 0, c == 7, ["wn", "xnT"], [pu])
            self.act(gates[:, tt_, :], pb[:, 0:24], AF.Sigmoid, [pu], ["gates"])
        if self.dbg.get("nsa_stop", 99) <= 2:
            P.barrier(); A.release(m0); self.nrr = 8; return
        for blk in range(T // 512):
            sl = slice(blk * 512, (blk + 1) * 512)
            proj(W_KC, blk, kcT[:, sl], "kcT")
            proj(W_VC, blk, vcT[:, sl], "vcT", eng="dve")
        P.barrier()
        A.release(m1b)
        w1 = [A.alloc([32, 128], BF16) for _ in range(2)]
        w2k = A.alloc([128], BF16)
        w2v = A.alloc([64], BF16)
        peT = A.alloc([2, 32], BF16)
        bias = A.alloc([2], F32)
        hid = A.alloc([2, 128], BF16)
        gu = [A.alloc([128], F32) for _ in range(3)]
        for i, (nm, pen) in enumerate((("nsa_ck1", "nsa_pe_k"), ("nsa_cv1", "nsa_pe_v"))):
            src = d[nm].rearrange("(l dd) j -> dd l j", dd=64)
            for rep in range(2):
                self.dma("pool", w1[i][rep * 64:(rep + 1) * 64, :, :], src, [], ["w1_%d" % i])
                self.dma("pool", peT[rep * 64:(rep + 1) * 64, i, :], d[pen].rearrange("l dd -> dd l"), [], ["peT"],
                         allow_slow_non_contiguous=True)
        for rep in range(2):
            self.dma("pool", w2k[:, rep * 64:(rep + 1) * 64], d["nsa_ck2"][:, :], [], ["w2k"])
        self.dma("pool", w2v, d["nsa_cv2"][:, :], [], ["w2v"])
        self.memset("pool", hid, 0.0, ["hid"])
        for i in range(2):
            _, pb, pu = self.bank()
            for l in range(32):
                self.mm(pb[:, 0:1], w1[i][0:64, l, :], peT[0:64, i, l:l + 1], l == 0, l == 31, ["w1_%d" % i, "peT"], [pu])
            self.copy("dve", bias[:, i:i + 1], pb[:, 0:1], [pu], ["cbias"])
        for g in range(2):
            gs = slice(g * 64, (g + 1) * 64)
            for i, srcT in enumerate((kcT, vcT)):
                _, pb, pu = self.bank()
                for l in range(32):
                    self.mm(pb[:, 0:NCMP], w1[i][gs, l, :], srcT[gs, l:l + 16 * (NCMP - 1) + 1:16], l == 0, l == 31,
                            ["w1_%d" % i, "kcT", "vcT"], [pu])
                u, u2, u3 = gu[0][:, 0:NCMP], gu[1][:, 0:NCMP], gu[2][:, 0:NCMP]
                self.act(u, pb[:, 0:NCMP], AF.Identity, [pu, "cbias"], ["gu0"], bias=bias[:, i:i + 1])
                self.tt("pool", u2, u, u, ALU.mult, ["gu0"], ["gu1"])
                self.ts("dve", u2, u2, 0.044715, 1.0, ALU.mult, ALU.add, ["gu1"], ["gu1"])
                self.tt("pool", u3, u2, u, ALU.mult, ["gu1", "gu0"], ["gu2"])
                self.act(u3, u3, AF.Sigmoid, ["gu2"], ["gu2"], scale=1.5957691216)
                self.tt("dve", hid[:, i, 0:NCMP], u3, u, ALU.mult, ["gu2", "gu0"], ["hid"])
            _, pb, pu = self.bank()
            self.mm(pb[:, 0:128], w2k, hid[:, 0, :], True, True, ["w2k", "hid"], [pu])
            self.copy("act", kc2[0:64, g, 0, :], pb[0:64, 0:128], [pu], ["kc2"])
            self.copy("act", kc2[64:128, g, 1, :], pb[64:128, 0:128], [pu], ["kc2"])
            _, pb, pu = self.bank()
            self.mm(pb[:, 0:64], hid[:, 1, :], w2v, True, True, ["w2v", "hid"], [pu])
            self.copy("dve", Vc[:, g, 0:64], pb[:, 0:64], [pu], ["Vc"])
        P.barrier()
        A.release(m1)
        if self.dbg.get("nsa_stop", 99) <= 3:
            A.release(m0); self.nrr = 8; return
        selT = A.alloc([2, T], BF16)
        cmpm = A.alloc([T], BF16)
        eexp = A.alloc([T], BF16)
        selv = A.alloc([NT, 32], F32)
        sela = A.alloc([NT, 32], F32)
        self.dma("pool", cmpm, d["c_cmpmask"][:, :], [], ["cmpm"])
        self.dma("pool", eexp, d["c_eexp"][:, :], [], ["eexp"])
        self.memset("pool", selT, 0.0, ["selTz"])
        self.dma("sp", selv, d["c_selvalid"][:, :, :], [], ["selv"])
        self.dma("sp", sela, d["c_seladd"][:, :, :], [], ["sela"])
        NPB = 4
        Pb = [A.alloc([512], BF16) for _ in range(NPB)]
        Pcb = [A.alloc([512], BF16) for _ in range(2)]
        mk = A.alloc([128], BF16)
        oaccs = [A.alloc([8, 64], F32) for _ in range(2)]
        obf = A.alloc([512], BF16)
        sms = [A.alloc([64], F32) for _ in range(3)]
        impns = [A.alloc([4, 32], F32) for _ in range(2)]
        imps = [A.alloc([32], F32) for _ in range(2)]
        sel01s = [A.alloc([32], BF16) for _ in range(2)]
        otmp = A.alloc([4, 64], F32)
        pk = [0]

        def bc4(ap128):
            return ap128.unsqueeze(1).to_broadcast([128, 4, 128])

        def v4(ap512):
            return ap512.rearrange("p (h c) -> p h c", h=4)

        def scores(KT2, kunit, QT, qunit, kt, g, qs):
            _, pb, pu = self.bank()
            for hh in range(4):
                e = hh % 2
                pair = g * 2 + hh // 2
                self.mm(pb[:, hh * 128:(hh + 1) * 128], KT2[:, g, e, kt * 128:(kt + 1) * 128], QT[:, pair, qs],
                        True, True, [kunit, qunit], [pu])
            return pb, pu

        def post(bo, pu, b, first, qt, g, sm, smu):
            oacc = oaccs[qt % 2]
            ou_ = "oacc%d" % (qt % 2)
            gate = gates[:, qt, g * 12:(g + 1) * 12].rearrange("p (h b) -> p h b", b=3)[:, :, b:b + 1]
            rd = sm[:, 0:4].unsqueeze(2)
            cf = sm[:, 4:8].unsqueeze(2)
            self.ts("dve", rd, bo[:, :, 64:65], 1e-30, None, ALU.max, None, [pu], [smu])
            self.recip(rd, rd, [smu], [smu])
            self.tt("dve", cf, rd, gate, ALU.mult, [smu, "gates"], [smu])
            og = oacc[:, g * 4:(g + 1) * 4, :]
            if first:
                self.tt("dve", og, bo[:, :, 0:64], cf.to_broadcast([128, 4, 64]), ALU.mult, [pu, smu], [ou_])
            else:
                self.tt("dve", otmp, bo[:, :, 0:64], cf.to_broadcast([128, 4, 64]), ALU.mult, [pu, smu], ["otmp"])
                self.tt("pool", og, og, otmp, ALU.add, ["otmp", ou_], [ou_])
            return rd

        def cmp_part(qt, g):
            qs = slice(qt * 128, (qt + 1) * 128)
            sm, smu = sms[g], "sm%d" % g
            impn, imp, sel01 = impns[g], imps[g], sel01s[g]
            iu, nu, su = "imp%d" % g, "impn%d" % g, "sel01%d" % g
            Pc, pcu = Pcb[g], "Pcb%d" % g
            pb, pu = scores(kc2, "kc2", qT, "qT", 0, g, qs)
            self.act(Pc, pb, AF.Exp, [pu], [pcu], scale=SC)
            self.tt("pool", v4(Pc), v4(Pc), bc4(cmpm[:, qs]), ALU.mult, [pcu, "cmpm"], [pcu])
            _, po, ou = self.bank()
            for hh in range(4):
                self.mm(po[:, hh * 97:(hh + 1) * 97], Pc[:, hh * 128:(hh + 1) * 128], Vc[:, g, :], True, True, [pcu, "Vc"], [ou])
            bo = po[:, 0:388].rearrange("p (h c) -> p h c", h=4)
            rd = post(bo, ou, 0, True, qt, g, sm, smu)
            self.tt("dve", impn, bo[:, :, 65:97], rd.to_broadcast([128, 4, 32]), ALU.mult, [ou, smu], [nu])
            self.P.op("dve", lambda e: e.tensor_reduce(out=imp, in_=impn.rearrange("p h c -> p c h"),
                                                       axis=mybir.AxisListType.X, op=ALU.add),
                      reads=[nu], writes=[iu])
            self.tt("pool", imp, imp, selv[:, qt, :], ALU.mult, [iu, "selv"], [iu])
            self.tt("pool", imp, imp, sela[:, qt, :], ALU.add, [iu, "sela"], [iu])
            self.P.op("dve", lambda e: e.max(out=sm[:, 8:16], in_=imp), reads=[iu], writes=[smu])
            self.ts("dve", sel01, imp, sm[:, 15:16], None, ALU.is_ge, None, [iu, smu], [su])
            _, pt, ptu = self.bank()
            ptb = pt.bitcast(BF16)
            self.tr(ptb[0:32, 0:128], sel01, self.identb, [su, "cbf"], [ptu])
            self.copy("act", selT[0:32, g, qs], ptb[0:32, 0:128], [ptu, "selTz"], ["selT%d_%d" % (g, qt)])

        def selwin_part(qt, g):
            qs = slice(qt * 128, (qt + 1) * 128)
            sm, smu = sms[2], "sm2"
            stu = "selT%d_%d" % (g, qt)
            _, pacc_s, accu_s = self.acc_bank()
            _, pacc_w, accu_w = self.acc_bank()
            k0 = max(0, qt - 4)
            tasks = [("sel", kt) for kt in range(qt + 1)] + [("win", kt) for kt in range(k0, qt + 1)]

            def stage_s(task):
                kind, kt = task
                Pq = Pb[pk[0] % NPB]; pqu = "Pb%d" % (pk[0] % NPB); pk[0] += 1
                if kind == "sel":
                    pb, pu = scores(ks2, "ks2", qrT, "qrT", kt, g, qs)
                    _, pm, pmu = self.bank()
                    self.mm(pm[:, 0:128], eexp[:, kt * 128:(kt + 1) * 128], selT[:, g, qs], True, True, ["eexp", stu, "selTz"], [pmu])
                    self.act(Pq, pb, AF.Exp, [pu], [pqu], scale=SC)
                    if kt == qt:
                        self.tt("dve", mk, pm[:, 0:128], self.m_iu, ALU.mult, [pmu, "cbf"], ["mk"])
                        self.tt("pool", v4(Pq), v4(Pq), bc4(mk), ALU.mult, [pqu, "mk"], [pqu])
                    else:
                        self.tt("dve", v4(Pq), v4(Pq), bc4(pm[:, 0:128]), ALU.mult, [pqu, pmu], [pqu])
                else:
                    pb, pu = scores(kw2, "kw2", qrT, "qrT", kt, g, qs)
                    self.act(Pq, pb, AF.Exp, [pu], [pqu], scale=SC)
                    if kt == qt:
                        self.tt("pool", v4(Pq), v4(Pq), bc4(self.m_iu), ALU.mult, [pqu, "cbf"], [pqu])
                    elif kt == qt - 4:
                        self.tt("pool", v4(Pq), v4(Pq), bc4(self.m_sl), ALU.mult, [pqu, "cbf"], [pqu])
                return Pq, pqu

            def stage_pv(task, Pq, pqu):
                kind, kt = task
                if kind == "sel":
                    pacc, accu, Vd, vu, first, b = pacc_s, accu_s, Vs, "Vs", 0, 1
                else:
                    pacc, accu, Vd, vu, first, b = pacc_w, accu_w, Vw, "Vw", k0, 2
                for hh in range(4):
                    self.mm(pacc[:, hh * 65:(hh + 1) * 65], Pq[:, hh * 128:(hh + 1) * 128], Vd[:, kt, g, :],
                            kt == first and hh == 0, kt == qt and hh == 3, [pqu, vu], [accu], skip_group_check=True)
                if kt == qt:
                    post(pacc[:, 0:260].rearrange("p (h c) -> p h c", h=4), accu, b, False, qt, g, sm, smu)

            LA = 2
            inflight = []
            for task in tasks:
                inflight.append((task,) + stage_s(task))
                if len(inflight) > LA:
                    stage_pv(*inflight.pop(0))
            while inflight:
                stage_pv(*inflight.pop(0))

        def fin_part(qt):
            qs = slice(qt * 128, (qt + 1) * 128)
            oacc = oaccs[qt % 2]
            self.copy("act", obf, oacc.rearrange("p h c -> p (h c)"), ["oacc%d" % (qt % 2)], ["obf"])
            _, pt, ptu = self.bank()
            ptb = pt.bitcast(BF16)
            for j in range(4):
                self.tr(ptb[:, j * 128:(j + 1) * 128], obf[:, j * 128:(j + 1) * 128], self.identb, ["obf", "cbf"], [ptu])
            self.copy("dve", self.onsT[:, :, qs], ptb[:, 0:512].rearrange("p (a b) -> p a b", a=4), [ptu], ["onsT"])

        def merge2(a_, b_):
            out = []
            ia = ib = 0
            na, nb = len(a_), len(b_)
            while ia < na or ib < nb:
                fa = ia / na if ia < na else 2.0
                fb = ib / nb if ib < nb else 2.0
                if fa <= fb:
                    out.append(a_[ia]); ia += 1
                else:
                    out.append(b_[ib]); ib += 1
            return out

        def cap(bs, fn):
            self.bank_set = bs
            r_ = P.capture(fn)
            self.bank_set = None
            return r_

        def cmp_both(qt):
            return merge2(cap([4], lambda: cmp_part(qt, 0)), cap([5], lambda: cmp_part(qt, 1)))

        P.replay(cmp_both(0))
        for qt in range(NT):
            o_main = cap([0, 1, 2, 3], lambda: (selwin_part(qt, 0), selwin_part(qt, 1), fin_part(qt)))
            o_bg = cmp_both(qt + 1) if qt + 1 < NT else []
            P.replay(merge2(o_main, o_bg))
        P.barrier()
        A.release(m0)
        self.nrr = 8


def host_consts(T):
    c = {}
    c["c_identf"] = np.eye(128, dtype=np.float32)
    bd = np.zeros((128, 128), np.float32)
    bd[:64, :64] = 1
    bd[64:, 64:] = 1
    c["c_bdf"] = bd
    s_ = np.arange(128)[:, None]
    t_ = np.arange(128)[None, :]
    perm = np.zeros((128, 128), np.float32)
    for m in range(128):
        src = m + 32 if (m % 64) < 32 else m - 32
        perm[src, m] = 1
    c["c_bf"] = np.concatenate([np.eye(128), bd, np.ones((128, 128)), (s_ < t_), (s_ > t_), (s_ <= t_), perm], axis=1).astype(np.float32)
    rst = np.ones((128, 512), np.float32)
    rst[:, ::128] = 0
    c["c_rst"] = rst
    half = 32
    inv = (10000.0 ** (-np.arange(half, dtype=np.float32) / half)).astype(np.float32)
    ang = (np.arange(T, dtype=np.float32)[None, :] * inv[:, None]).astype(np.float32)
    cos = np.cos(ang).astype(np.float32)
    sin = np.sin(ang).astype(np.float32)
    cos64 = np.concatenate([cos, cos], 0)
    sin64 = np.concatenate([-sin, sin], 0)
    c["c_cos"] = np.concatenate([cos64, cos64], 0)
    c["c_sin"] = np.concatenate([sin64, sin64], 0)
    n = np.arange(128)[:, None]
    tq = np.arange(T)[None, :]
    cm = ((16 * n + 31) <= tq) & (n < (T - 32) // 16 + 1)
    c["c_cmpmask"] = cm.astype(np.float32)
    j = np.arange(32)[:, None]
    kpos = np.arange(T)[None, :]
    ee = np.zeros((128, T), np.float32)
    ee[:32] = (j == kpos // 64)
    c["c_eexp"] = ee
    tt = np.arange(T)
    cur = tt // 64
    jj = np.arange(32)[None, :]
    forced = (jj == 0) | (jj == cur[:, None]) | (jj == cur[:, None] - 1)
    valid = (jj <= cur[:, None]) & ~forced
    add = np.where(forced, 1e4, np.where(jj <= cur[:, None], 0.0, -1.0))
    c["c_selvalid"] = valid.astype(np.float32).reshape(T // 128, 128, 32).transpose(1, 0, 2).copy()
    c["c_seladd"] = add.astype(np.float32).reshape(T // 128, 128, 32).transpose(1, 0, 2).copy()
    n_cmp = (T - 32) // 16 + 1
    cs = np.arange(n_cmp) * 16
    ss = np.arange(T // 64) * 64
    ov = np.clip(np.minimum(cs[:, None] + 32, ss[None, :] + 64) - np.maximum(cs[:, None], ss[None, :]), 0, None) / 32.0
    ovl = np.zeros((128, 32), np.float32)
    ovl[:n_cmp, :T // 64] = ov
    c["c_ovl"] = ovl
    return c


def host_cols(inp):
    cols = np.zeros((128, NCOLS), np.float32)

    def put(base, v):
        v = np.asarray(v, np.float32).reshape(-1)
        n = v.size // 128
        cols[:, base:base + n] = v.reshape(n, 128).T

    put(CG_MIX, inp["norm_mix_g"]); put(CG_XA, inp["norm_xa_g"]); put(CG_MEM, inp["norm_mem_g"])
    put(CG_FFN, inp["norm_ffn_g"]); put(CG_FIN, inp["final_norm_g"]); put(C_MU, inp["shift_mu"])
    put(C_W0, inp["rw_w0"]); put(C_A0, inp["rw_a0"]); put(C_KK, inp["rw_k_k"]); put(C_KA, inp["rw_k_a"])
    put(C_RK, inp["rw_r_k"]); put(C_LNG, inp["rw_ln_g"]); put(C_LNB, inp["rw_ln_b"])
    return cols


W_NAMES = ["w_in", "rw_w_up", "rw_a_up", "rw_g_up", "nsa_pe_k", "nsa_pe_v", "nsa_ck1", "nsa_ck2", "nsa_cv1", "nsa_cv2",
           "w_up_rw", "w_up_nsa", "w_out", "xa_wq", "xa_wkv", "xa_wo", "ffn_w_gu", "ffn_w_down"]


def make_in_maps(inputs, T, NSEQ, ncores, extra=None):
    consts = host_consts(T)
    cols = host_cols(inputs)
    shared = {k: np.ascontiguousarray(np.asarray(inputs[k], np.float32)[0]) for k in W_NAMES}
    shared.update(consts)
    shared["cols"] = cols
    x = np.asarray(inputs["x"], np.float32)
    mem = np.asarray(inputs["mem"], np.float32)
    maps = []
    for i in range(ncores):
        m = dict(shared)
        m["x"] = np.ascontiguousarray(x[i * NSEQ:(i + 1) * NSEQ])
        m["mem"] = np.ascontiguousarray(mem[i * NSEQ:(i + 1) * NSEQ])
        if extra:
            for k, v in extra.items():
                m[k] = np.ascontiguousarray(v[i * NSEQ:(i + 1) * NSEQ])
        maps.append(m)
    return maps


def kernel(**inputs):
    B, T, _ = inputs["x"].shape
    ncores = 8
    NSEQ = B // ncores
    nc = bass.Bass("TRN2", target_bir_lowering=False)
    bld = Builder(nc, T, NSEQ)
    bld.build()
    maps = make_in_maps(inputs, T, NSEQ, ncores)
    res = run_bass_kernel_spmd(nc, maps, core_ids=list(range(ncores)))
    out = np.concatenate([r["out"] for r in res.results], axis=0)
    return out.astype(np.float32)
```
